# Optimizing a Trainium2 kernel written in Bass

```python
import math
import jax
import jax.numpy as jnp
from jax import lax
import numpy as np

D_MODEL = 1024
BATCH = 16
SEQ = 4096
DEPTH = 1
DEC_BATCH = 8
DEC_SEQ = 2048
PAST_LEN = 128

N_META = 16
NORM_EPS = 1e-6
SSM_WIDTH = D_MODEL // 2
SSM_GROUP = 16
SSM_GROUPS = SSM_WIDTH // SSM_GROUP
SSM_STATE = 64
SSM_DT_MIN = 1e-3
SSM_DT_MAX = 1e-1
RET_HEADS = 4
RET_QK_DIM = 128
RET_V_DIM = 2 * RET_QK_DIM
RET_QK_WIDTH = RET_HEADS * RET_QK_DIM
RET_V_WIDTH = RET_HEADS * RET_V_DIM
RET_CHUNK = 128
ROPE_BASE = 10000.0
D_FF = ((-(-8 * D_MODEL // 3) + 255) // 256) * 256
IN_SPLITS = (SSM_WIDTH, RET_QK_WIDTH, RET_QK_WIDTH, RET_V_WIDTH, RET_V_WIDTH, D_MODEL, D_MODEL)
IN_WIDTH = sum(IN_SPLITS)
IN_SPLIT_POINTS = tuple(sum(IN_SPLITS[:i + 1]) for i in range(len(IN_SPLITS) - 1))

kernel_name = "hybrid_s5_retention_encoder"


def _rmsnorm(x, g):
    xf = x.astype(jnp.float32)
    return xf * lax.rsqrt(jnp.mean(xf * xf, axis=-1, keepdims=True) + NORM_EPS) * g.astype(jnp.float32)


def _rope(x, pos):
    half = x.shape[-1] // 2
    inv = ROPE_BASE ** (-jnp.arange(half, dtype=jnp.float32) / half)
    ang = pos[:, None] * inv[None, :]
    cos = jnp.cos(ang)[None, :, None, :]
    sin = jnp.sin(ang)[None, :, None, :]
    x1, x2 = x[..., :half], x[..., half:]
    return jnp.concatenate([x1 * cos - x2 * sin, x1 * sin + x2 * cos], axis=-1)


def _cplx_scan_combine(e1, e2):
    a1r, a1i, b1r, b1i = e1
    a2r, a2i, b2r, b2i = e2
    return (a1r * a2r - a1i * a2i,
            a1r * a2i + a1i * a2r,
            a2r * b1r - a2i * b1i + b2r,
            a2r * b1i + a2i * b1r + b2i)


def _s5_branch(u, lam_re, lam_im, log_step, b_re, b_im, c_re, c_im, d_skip):
    f32 = jnp.float32
    bsz, seq_len, _ = u.shape
    ug = u.astype(f32).reshape(bsz, seq_len, SSM_GROUPS, SSM_GROUP)
    y = ug * d_skip.astype(f32).reshape(SSM_GROUPS, SSM_GROUP)
    a_shape = (1, seq_len, SSM_GROUPS, SSM_STATE)
    for direction in range(2):
        lr = lam_re[direction].astype(f32)
        li = lam_im[direction].astype(f32)
        dt = jnp.exp(log_step[direction].astype(f32))[:, None]
        mag = jnp.exp(lr * dt)
        ab_re = mag * jnp.cos(li * dt)
        ab_im = mag * jnp.sin(li * dt)
        den = lr * lr + li * li
        nr = ab_re - 1.0
        coef_re = (nr * lr + ab_im * li) / den
        coef_im = (ab_im * lr - nr * li) / den
        br = b_re[direction].astype(f32)
        bi = b_im[direction].astype(f32)
        bb_re = coef_re[..., None] * br - coef_im[..., None] * bi
        bb_im = coef_re[..., None] * bi + coef_im[..., None] * br
        bu_re = jnp.einsum('blgh,gph->blgp', ug, bb_re)
        bu_im = jnp.einsum('blgh,gph->blgp', ug, bb_im)
        a_re = jnp.broadcast_to(ab_re, a_shape)
        a_im = jnp.broadcast_to(ab_im, a_shape)
        _, _, s_re, s_im = lax.associative_scan(
            _cplx_scan_combine, (a_re, a_im, bu_re, bu_im), reverse=(direction == 1), axis=1)
        y = (y + jnp.einsum('blgp,ghp->blgh', s_re, c_re[direction].astype(f32))
             - jnp.einsum('blgp,ghp->blgh', s_im, c_im[direction].astype(f32)))
    return y.reshape(bsz, seq_len, SSM_WIDTH)


def _decay_matrix(log_gf, log_gb, n):
    i = jnp.arange(n)[:, None]
    j = jnp.arange(n)[None, :]
    df = jnp.maximum(i - j, 0).astype(jnp.float32)
    db = jnp.maximum(j - i, 0).astype(jnp.float32)
    return jnp.where(i >= j,
                     jnp.exp(log_gf[:, None, None] * df),
                     jnp.exp(log_gb[:, None, None] * db))


def _retention(q, k, v, log_gf, log_gb):
    bsz, seq_len = q.shape[:2]
    C = RET_CHUNK
    n_chunks = (seq_len - N_META) // C
    qm, km, vm = q[:, :N_META], k[:, :N_META], v[:, :N_META]

    def chunked(t):
        return t[:, N_META:].reshape(bsz, n_chunks, C, RET_HEADS, t.shape[-1])

    qr, kr, vr = chunked(q), chunked(k), chunked(v)
    s_real = jnp.einsum('bnchd,bnshd->bnhcs', qr, kr) * _decay_matrix(log_gf, log_gb, C)
    o_real = jnp.einsum('bnhcs,bnshe->bnche', s_real, vr)
    s_meta = jnp.einsum('bchd,bshd->bhcs', qm, km) * _decay_matrix(log_gf, log_gb, N_META)
    o_meta = jnp.einsum('bhcs,bshe->bche', s_meta, vm)

    pos_c = jnp.arange(C, dtype=jnp.float32)[:, None]
    pos_m = jnp.arange(N_META, dtype=jnp.float32)[:, None]
    xs = (jnp.moveaxis(qr, 1, 0), jnp.moveaxis(kr, 1, 0), jnp.moveaxis(vr, 1, 0))

    wq_f = jnp.exp(log_gf * (pos_c + 1.0))
    wk_f = jnp.exp(log_gf * (C - 1.0 - pos_c))
    carry_f = jnp.exp(log_gf * C)[None, :, None, None]
    state0 = jnp.einsum('bshd,bshe,sh->bhde', km, vm, jnp.exp(log_gf * (N_META - 1.0 - pos_m)))

    def fwd_step(state, chunk):
        qn, kn, vn = chunk
        out = jnp.einsum('bchd,bhde,ch->bche', qn, state, wq_f)
        state = carry_f * state + jnp.einsum('bshd,bshe,sh->bhde', kn, vn, wk_f)
        return state, out

    _, o_fwd = lax.scan(fwd_step, state0, xs)

    wq_b = jnp.exp(log_gb * (C - pos_c))
    wk_b = jnp.exp(log_gb * pos_c)
    carry_b = jnp.exp(log_gb * C)[None, :, None, None]

    def bwd_step(state, chunk):
        qn, kn, vn = chunk
        out = jnp.einsum('bchd,bhde,ch->bche', qn, state, wq_b)
        state = jnp.einsum('bshd,bshe,sh->bhde', kn, vn, wk_b) + carry_b * state
        return state, out

    state_meta, o_bwd = lax.scan(bwd_step, jnp.zeros_like(state0), xs, reverse=True)
    o_meta = o_meta + jnp.einsum('bchd,bhde,ch->bche', qm, state_meta,
                                 jnp.exp(log_gb * (N_META - pos_m)))
    o_real = o_real + jnp.moveaxis(o_fwd + o_bwd, 0, 1)
    return jnp.concatenate(
        [o_meta, o_real.reshape(bsz, n_chunks * C, RET_HEADS, RET_V_DIM)], axis=1)


def _encoder(x, meta_tokens, norm_mix_g, w_in, ssm_lam_re, ssm_lam_im, ssm_log_step,
             ssm_b_re, ssm_b_im, ssm_c_re, ssm_c_im, ssm_d, w_ssm_glu, ret_decay_logit,
             w_ret_out, w_out, norm_ffn_g, w_ffn_in, w_ffn_out, norm_final_g):
    f32 = jnp.float32
    bsz, seq, _ = x.shape
    seq_len = N_META + seq
    h = jnp.concatenate(
        [jnp.broadcast_to(meta_tokens.astype(f32)[None], (bsz, N_META, D_MODEL)), x.astype(f32)], axis=1)
    pos = jnp.arange(seq_len, dtype=f32)
    for layer in range(DEPTH):
        n1 = _rmsnorm(h, norm_mix_g[layer])
        proj = n1 @ w_in[layer].astype(f32)
        u, q, k, v, g_ret, gate_a, gate_b = jnp.split(proj, IN_SPLIT_POINTS, axis=-1)
        y_ssm = jax.nn.gelu(_s5_branch(u, ssm_lam_re[layer], ssm_lam_im[layer], ssm_log_step[layer],
                                       ssm_b_re[layer], ssm_b_im[layer], ssm_c_re[layer],
                                       ssm_c_im[layer], ssm_d[layer]))
        a_val, a_gate = jnp.split(y_ssm @ w_ssm_glu[layer].astype(f32), 2, axis=-1)
        branch_a = a_val * jax.nn.sigmoid(a_gate)
        q = _rope(q.reshape(bsz, seq_len, RET_HEADS, RET_QK_DIM), pos) * (RET_QK_DIM ** -0.5)
        k = _rope(k.reshape(bsz, seq_len, RET_HEADS, RET_QK_DIM), pos)
        v = v.reshape(bsz, seq_len, RET_HEADS, RET_V_DIM)
        log_g = jax.nn.log_sigmoid(ret_decay_logit[layer].astype(f32))
        o = _retention(q, k, v, log_g[0], log_g[1])
        o = o * lax.rsqrt(jnp.mean(o * o, axis=-1, keepdims=True) + NORM_EPS)
        o = o.reshape(bsz, seq_len, RET_V_WIDTH) * jax.nn.silu(g_ret)
        branch_b = o @ w_ret_out[layer].astype(f32)
        mixed = jax.nn.sigmoid(gate_a) * branch_a + jax.nn.sigmoid(gate_b) * branch_b
        h = h + mixed @ w_out[layer].astype(f32)
        n2 = _rmsnorm(h, norm_ffn_g[layer])
        ff_gate, ff_up = jnp.split(n2 @ w_ffn_in[layer].astype(f32), 2, axis=-1)
        h = h + (jax.nn.silu(ff_gate) * ff_up) @ w_ffn_out[layer].astype(f32)
    out = _rmsnorm(h, norm_final_g)
    return out[:, N_META:].astype(x.dtype)


def setup_inputs(seed: int = 0) -> dict:
    key = jax.random.key(seed)
    ks = jax.random.split(key, 24)
    f32 = jnp.float32

    def normal(k, shape):
        return jax.random.normal(k, shape, f32)

    def dense(k, fan_in, fan_out):
        return normal(k, (DEPTH, fan_in, fan_out)) * fan_in ** -0.5

    def gain(k, shape):
        return 1.0 + 0.02 * normal(k, shape)

    ssm_shape = (DEPTH, 2, SSM_GROUPS, SSM_STATE)
    b_shape = (DEPTH, 2, SSM_GROUPS, SSM_STATE, SSM_GROUP)
    c_shape = (DEPTH, 2, SSM_GROUPS, SSM_GROUP, SSM_STATE)
    n_idx = jnp.arange(SSM_STATE, dtype=f32)
    heads = jnp.arange(RET_HEADS, dtype=f32)
    decay_init = jnp.log(2.0 ** (5.0 + heads) - 1.0)
    return {
        "x_prompt": normal(ks[0], (BATCH, SEQ, D_MODEL)),
        "x_sample": normal(ks[1], (DEC_BATCH, DEC_SEQ, D_MODEL)),
        "meta_tokens": 0.5 * normal(ks[2], (N_META, D_MODEL)),
        "norm_mix_g": gain(ks[3], (DEPTH, D_MODEL)),
        "w_in": dense(ks[4], D_MODEL, IN_WIDTH),
        "ssm_lam_re": -0.5 + 0.01 * normal(ks[5], ssm_shape),
        "ssm_lam_im": jnp.pi * n_idx + 0.01 * normal(ks[6], ssm_shape),
        "ssm_log_step": jax.random.uniform(ks[7], (DEPTH, 2, SSM_GROUPS), f32,
                                           math.log(SSM_DT_MIN), math.log(SSM_DT_MAX)),
        "ssm_b_re": normal(ks[8], b_shape) * (2 * SSM_GROUP) ** -0.5,
        "ssm_b_im": normal(ks[9], b_shape) * (2 * SSM_GROUP) ** -0.5,
        "ssm_c_re": 0.5 * normal(ks[10], c_shape),
        "ssm_c_im": 0.5 * normal(ks[11], c_shape),
        "ssm_d": normal(ks[12], (DEPTH, SSM_WIDTH)),
        "w_ssm_glu": dense(ks[13], SSM_WIDTH, 2 * D_MODEL),
        "ret_decay_logit": decay_init + 0.01 * normal(ks[14], (DEPTH, 2, RET_HEADS)),
        "w_ret_out": dense(ks[15], RET_V_WIDTH, D_MODEL),
        "w_out": dense(ks[16], D_MODEL, D_MODEL),
        "norm_ffn_g": gain(ks[17], (DEPTH, D_MODEL)),
        "w_ffn_in": dense(ks[18], D_MODEL, 2 * D_FF),
        "w_ffn_out": dense(ks[19], D_FF, D_MODEL),
        "norm_final_g": gain(ks[20], (D_MODEL,)),
    }


def reference(x_prompt, x_sample, meta_tokens, norm_mix_g, w_in, ssm_lam_re, ssm_lam_im,
              ssm_log_step, ssm_b_re, ssm_b_im, ssm_c_re, ssm_c_im, ssm_d, w_ssm_glu,
              ret_decay_logit, w_ret_out, w_out, norm_ffn_g, w_ffn_in, w_ffn_out, norm_final_g):
    params = (meta_tokens, norm_mix_g, w_in, ssm_lam_re, ssm_lam_im, ssm_log_step,
              ssm_b_re, ssm_b_im, ssm_c_re, ssm_c_im, ssm_d, w_ssm_glu, ret_decay_logit,
              w_ret_out, w_out, norm_ffn_g, w_ffn_in, w_ffn_out, norm_final_g)
    y_prompt = _encoder(x_prompt, *params)
    y_sample = _encoder(x_sample, *params)
    return (y_prompt, y_sample)
```

```python
import math
from contextlib import ExitStack

import numpy as np
import concourse.bass as bass
import concourse.mybir as mybir
from concourse.bass_utils import run_bass_kernel_spmd

F32 = mybir.dt.float32
BF16 = mybir.dt.bfloat16
I32 = mybir.dt.int32
AF = mybir.ActivationFunctionType
ALU = mybir.AluOpType
AX = mybir.AxisListType

D = 1024
KT = 8
NM = 16
CH = 128
H = 4
DK = 128
DV = 256
NG = 32
GC = 16
PS = 64
DFF = 2816
FT = 22
NT = 8
EPS = 1e-6
TWO_PI = 2.0 * math.pi
C1 = 6.28125
C2 = TWO_PI - 6.28125
OFF_U, OFF_Q, OFF_K, OFF_V, OFF_GR, OFF_GA, OFF_GB = 0, 512, 1024, 1536, 2560, 3584, 4608
N_WS = 26


class T:
    __slots__ = ("ap", "w", "r", "dsem", "dcnt", "closed", "key")

    def __init__(self, ap, key=None):
        self.ap = ap
        self.key = key
        self.w = None
        self.r = {}
        self.dsem = None
        self.dcnt = 0
        self.closed = None


class KB:
    def __init__(self, nc, es):
        self.nc = nc
        self.es = es
        self.E = {"pe": nc.tensor, "act": nc.scalar, "dve": nc.vector, "pool": nc.gpsimd, "sp": nc.sync}
        self.nsem = 0
        self.sem = {}
        self.cnt = {}
        for e in ("pe", "act", "dve", "pool"):
            self._new_epoch(e)
        self.seen = {e: {} for e in self.E}
        self.dma_evs = []
        self.out_evs = []
        self.nwaits = 0
        self.dsem_pool = {}

    def new_sem(self, name):
        self.nsem += 1
        return self.es.enter_context(self.nc.semaphore(f"{name}{self.nsem}"))

    def _new_epoch(self, e):
        self.sem[e] = self.new_sem("c" + e)
        self.cnt[e] = 0

    def _wait(self, eng, ev):
        sem, val, _, owner = ev
        if owner is not None:
            val = owner.dcnt
            owner.closed = val
        s = self.seen[eng]
        if s.get(id(sem), 0) >= val:
            return
        s[id(sem)] = val
        self.nwaits += 1
        self.E[eng].wait_ge(sem, val)

    def op(self, eng, fn, reads=(), writes=()):
        evs = []
        for t in reads:
            if t.w is not None:
                evs.append(t.w)
        for t in writes:
            if t.w is not None:
                evs.append(t.w)
            evs.extend(t.r.values())
        for ev in evs:
            self._wait(eng, ev)
        inst = fn(self.E[eng])
        if self.cnt[eng] >= 30000:
            self._new_epoch(eng)
        self.cnt[eng] += 1
        inst.then_inc(self.sem[eng], 1)
        ev = (self.sem[eng], self.cnt[eng], eng, None)
        for t in reads:
            t.r[eng] = ev
        for t in writes:
            t.w = ev
            t.r = {}
        return ev

    def dma(self, q, out_ap, in_ap, reads=(), writes=(), owner=None, is_out=False, group=False, **kw):
        owner = owner or (writes[0] if writes else reads[0])
        if owner.dsem is None:
            if owner.key is not None and owner.key in self.dsem_pool:
                owner.dsem, owner.dcnt = self.dsem_pool[owner.key]
            else:
                owner.dsem = self.new_sem("d")
                owner.dcnt = 0
        src = ("dma", id(owner.dsem))
        evs = []
        for t in reads:
            if t.w is not None:
                evs.append(t.w)
        for t in writes:
            if t.w is not None and not (group and t.w[2] == src):
                evs.append(t.w)
            for s2, ev in t.r.items():
                if not (group and s2 == src):
                    evs.append(ev)
        for ev in evs:
            self._wait(q, ev)
        if owner.closed is not None:
            self._wait(q, (owner.dsem, owner.closed, src, owner))
            owner.closed = None
        inst = self.E[q].dma_start(out=out_ap, in_=in_ap, **kw)
        owner.dcnt += 16
        if owner.key is not None:
            self.dsem_pool[owner.key] = (owner.dsem, owner.dcnt)
        inst.then_inc(owner.dsem, 16)
        ev = (owner.dsem, owner.dcnt, src, owner)
        for t in reads:
            t.r[src] = ev
        for t in writes:
            t.w = ev
            t.r = {}
        self.dma_evs.append(ev)
        if is_out:
            self.out_evs.append(ev)
        return ev

    def barrier(self):
        evs = [(self.sem[e], self.cnt[e], e, None) for e in ("pe", "act", "dve", "pool") if self.cnt[e] > 0]
        last = {}
        for ev in self.dma_evs:
            last[ev[2]] = ev
        evs += list(last.values())
        for eng in ("pe", "act", "dve", "pool", "sp"):
            for ev in evs:
                if ev[2] != eng:
                    self._wait(eng, ev)
        self.dma_evs = []

    def finish(self):
        last = {}
        for ev in self.out_evs:
            last[ev[2]] = ev
        for ev in last.values():
            self._wait("sp", ev)


def make_consts(nch_max, nsb_max):
    c = {}
    c["ident"] = np.eye(128, dtype=np.float32)
    pr = np.arange(128)
    pos = np.zeros((128, nch_max + 1), np.float32)
    for n in range(nch_max):
        pos[:, n] = NM + CH * n + pr
    pos[:, nch_max] = pr
    c["pos"] = pos
    inv = (np.float32(10000.0) ** (-np.arange(64, dtype=np.float32) / np.float32(64))).astype(np.float32)
    c["fidx"] = np.tile(inv[None, :], (128, 1))
    c["sbidx"] = np.tile(np.arange(nsb_max, dtype=np.float32)[None, :], (128, 1))
    ii = pr // GC
    c["maskf"] = (ii[None, :] >= ii[:, None]).astype(np.float32)
    c["maskb"] = (ii[:, None] >= ii[None, :]).astype(np.float32)
    sp_, cc = pr[:, None], pr[None, :]
    c["dmf"] = (cc >= sp_).astype(np.float32)
    c["dmb"] = (cc < sp_).astype(np.float32)
    c["ef"] = np.maximum(cc - sp_, 0).astype(np.float32)
    c["eb"] = np.maximum(sp_ - cc, 0).astype(np.float32)
    misc = np.zeros((128, 8), np.float32)
    misc[:, 0] = pr + 1.0
    misc[:, 1] = CH - 1.0 - pr
    misc[:, 2] = CH - pr
    misc[:, 3] = pr
    misc[:, 4] = NM - 1.0 - pr
    misc[:, 5] = np.where(pr < 64, 1.0, -1.0)
    misc[:, 6] = -misc[:, 5]
    misc[:, 7] = float(CH)
    c["misc"] = misc
    kexp = np.tile(np.arange(-7, 9, dtype=np.float32)[None, :], (128, 1))
    c["kexp"] = kexp
    return c


CONST_SHAPES = None


def build(NP_, SP_, NS_, SS_, dbg=None):
    dbg = dbg or set()
    nc = bass.Bass("TRN2", target_bir_lowering=False)
    nch_max = max(SP_ if NP_ else 0, SS_ if NS_ else 0) // CH
    nsb_max = (max(SP_ if NP_ else 0, SS_ if NS_ else 0) + NM) // NT
    consts = make_consts(nch_max, nsb_max)
    es = ExitStack()
    with es:
        kb = KB(nc, es)

        def din(name, shape, dt=F32):
            return nc.dram_tensor(name, list(shape), dt, kind="ExternalInput").ap()

        def dout(name, shape, dt=F32):
            return nc.dram_tensor(name, list(shape), dt, kind="ExternalOutput").ap()

        def dscr(name, shape, dt):
            if "scratch" in dbg:
                return nc.dram_tensor(name, list(shape), dt, kind="ExternalOutput").ap()
            return nc.dram_tensor(name, list(shape), dt).ap()

        uniq = [0]

        def sbt(stack, name, shape, dt):
            uniq[0] += 1
            return stack.enter_context(nc.sbuf_tensor(f"{name}_{uniq[0]}", list(shape), dt))

        I = {}
        if NP_:
            I["xp"] = din("xp", [NP_, SP_, D])
            yp = dout("yp", [NP_, SP_, D])
        if NS_:
            I["xs"] = din("xs", [NS_, SS_, D])
            ys = dout("ys", [NS_, SS_, D])
        I["meta"] = din("meta", [NM, D])
        I["g_mix"] = din("g_mix", [D])
        I["w_in"] = din("w_in", [D, 5632])
        I["lam_re"] = din("lam_re", [2, NG, PS])
        I["lam_im"] = din("lam_im", [2, NG, PS])
        I["log_step"] = din("log_step", [2, NG])
        I["b_re"] = din("b_re", [2, NG, PS, GC])
        I["b_im"] = din("b_im", [2, NG, PS, GC])
        I["c_re"] = din("c_re", [2, NG, GC, PS])
        I["c_im"] = din("c_im", [2, NG, GC, PS])
        I["ssm_d"] = din("ssm_d", [512])
        I["w_glu"] = din("w_glu", [512, 2048])
        I["decay"] = din("decay", [8])
        I["w_ro"] = din("w_ro", [D, D])
        I["w_out"] = din("w_out", [D, D])
        I["g_ffn"] = din("g_ffn", [D])
        I["w_f1"] = din("w_f1", [D, 2 * DFF])
        I["w_f2"] = din("w_f2", [DFF, D])
        I["g_fin"] = din("g_fin", [D])
        CI = {k: din("c_" + k, v.shape) for k, v in consts.items()}

        wS = dscr("wS", [N_WS, 128, 8 * 512], BF16)
        wM = dscr("wM", [8, 128, 4096], BF16)
        s5w = dscr("s5w", [64, 128, 512], BF16)
        s5m = dscr("s5m", [NG, 128, 128], BF16)
        rot = dscr("rot", [64, 128, 2, nsb_max], F32)
        DBG = {}

        def dbg_out(name, src_ap, shape, reads, dt=F32):
            if name not in dbg:
                return
            o = dout("dbg_" + name, shape, dt)
            DBG[name] = o
            kb.dma("sp", o, src_ap, reads=reads, is_out=True)

        top = ExitStack()
        es.enter_context(top)
        psum = [T(es.enter_context(nc.psum_tensor(f"ps{b}", [128, 512], F32))[:]) for b in range(8)]
        ps_rr = [0]

        def ps_next():
            b = ps_rr[0] % 8
            ps_rr[0] += 1
            return psum[b]

        def ps_pair():
            if ps_rr[0] % 2:
                ps_rr[0] += 1
            b = ps_rr[0] % 8
            ps_rr[0] += 2
            return psum[b], psum[b + 1]

        def ctile(name, shape, dt=F32):
            return T(sbt(top, name, shape, dt)[:])

        ident_f = ctile("ident_f", [128, 128])
        ident_b = ctile("ident_b", [128, 128], BF16)
        misc = ctile("misc", [128, 8])
        ropec = ctile("ropec", [128, (nch_max + 1) * 64])
        ropes = ctile("ropes", [128, (nch_max + 1) * 64])
        dmT = ctile("dmT", [128, 512])
        wtab = ctile("wtab", [128, 32])
        gfin = ctile("gfin", [128, D])
        st0 = ctile("st0", [128, H * DV])
        umeta = ctile("umeta", [128, 4 * NM], BF16)
        cload = T(None)

        def cdma(dst_t, dst_ap, src_ap, **kw):
            kb.dma("sp", dst_ap, src_ap, writes=[dst_t], owner=cload, group=True, **kw)

        cdma(ident_f, ident_f.ap, CI["ident"])
        cdma(misc, misc.ap, CI["misc"])
        cdma(gfin, gfin.ap, I["g_fin"].partition_broadcast(128))
        kb.op("dve", lambda e: e.tensor_copy(ident_b.ap, ident_f.ap), [ident_f], [ident_b])

        def sincos(eng_v, ang, n, sin_out, cos_out, tmp_f, tmp_i, sin_scale=None):
            a = ang.ap
            tf, ti = tmp_f.ap, tmp_i.ap
            kb.op(eng_v, lambda e: e.tensor_scalar(tf, a, 1.0 / TWO_PI, None, ALU.mult), [ang], [tmp_f])
            kb.op(eng_v, lambda e: e.tensor_copy(ti, tf), [tmp_f], [tmp_i])
            kb.op(eng_v, lambda e: e.tensor_copy(tf, ti), [tmp_i], [tmp_f])
            kb.op(eng_v, lambda e: e.scalar_tensor_tensor(a, tf, -C1, a, ALU.mult, ALU.add), [tmp_f, ang], [ang])
            kb.op(eng_v, lambda e: e.scalar_tensor_tensor(a, tf, -C2, a, ALU.mult, ALU.add), [tmp_f, ang], [ang])
            kb.op(eng_v, lambda e: e.tensor_scalar(a, a, math.pi, -math.pi, ALU.min, ALU.max), [ang], [ang])
            if sin_out is not None:
                if sin_scale is None:
                    kb.op("act", lambda e: e.activation(sin_out.ap, a, AF.Sin), [ang], [sin_out])
                else:
                    kb.op("act", lambda e: e.activation(tf, a, AF.Sin), [ang], [tmp_f])
                    kb.op("act", lambda e: e.activation(sin_out.ap, tf, AF.Copy, scale=sin_scale[1]),
                          [tmp_f, sin_scale[0]], [sin_out])
            kb.op(eng_v, lambda e: e.tensor_scalar(tf, a, math.pi / 2, TWO_PI, ALU.is_gt, ALU.mult), [ang], [tmp_f])
            kb.op(eng_v, lambda e: e.scalar_tensor_tensor(a, a, math.pi / 2, tf, ALU.add, ALU.subtract), [ang, tmp_f], [ang])
            kb.op(eng_v, lambda e: e.tensor_scalar(a, a, math.pi, -math.pi, ALU.min, ALU.max), [ang], [ang])
            kb.op("act", lambda e: e.activation(cos_out.ap, a, AF.Sin), [ang], [cos_out])

        with ExitStack() as ph:
            def ptile(name, shape, dt=F32):
                return T(sbt(ph, name, shape, dt)[:])
            ncol = (nch_max + 1) * 64
            pos = ptile("pos", [128, nch_max + 1])
            fidx = ptile("fidx", [128, 64])
            inv = ptile("inv", [128, 64])
            ang = ptile("ang", [128, ncol])
            tf = ptile("tf", [128, ncol])
            ti = ptile("ti", [128, ncol], I32)
            cdma(pos, pos.ap, CI["pos"])
            cdma(fidx, fidx.ap, CI["fidx"])
            kb.op("dve", lambda e: e.tensor_copy(inv.ap, fidx.ap), [fidx], [inv])
            a3 = ang.ap.rearrange("p (n f) -> p n f", f=64)
            kb.op("dve", lambda e: e.tensor_tensor(
                a3, inv.ap.unsqueeze(1).broadcast_to([128, nch_max + 1, 64]),
                pos.ap.unsqueeze(2).broadcast_to([128, nch_max + 1, 64]), ALU.mult), [inv, pos], [ang])
            sincos("dve", ang, ncol, ropes, ropec, tf, ti)
            dl = ptile("dl", [128, 8])
            lg = ptile("lg", [128, 8])
            cdma(dl, dl.ap, I["decay"].partition_broadcast(128))
            kb.op("act", lambda e: e.activation(lg.ap, dl.ap, AF.Exp, scale=-1.0), [dl], [lg])
            kb.op("act", lambda e: e.activation(lg.ap, lg.ap, AF.Ln, bias=1.0), [lg], [lg])
            kb.op("dve", lambda e: e.tensor_scalar(lg.ap, lg.ap, -1.0, None, ALU.mult), [lg], [lg])
            wt = wtab.ap
            m = misc.ap
            specs = [(0, 0, 0), (1, 0, 1), (2, 4, 2), (3, 4, 3), (4, 0, 7), (5, 4, 7), (6, 0, 4)]
            for (slot, lgo, mcol) in specs:
                kb.op("dve", lambda e, slot=slot, lgo=lgo, mcol=mcol: e.tensor_scalar(
                    wt[:, slot * 4:(slot + 1) * 4], lg.ap[:, lgo:lgo + 4], m[:, mcol:mcol + 1], None, ALU.mult),
                    [lg, misc], [wtab])
            kb.op("act", lambda e: e.activation(wt[:, 0:28], wt[:, 0:28], AF.Exp), [wtab], [wtab])
            ef = ptile("ef", [128, 128]); eb = ptile("eb", [128, 128])
            dmf = ptile("dmf", [128, 128]); dmb = ptile("dmb", [128, 128])
            t1 = ptile("t1", [128, 128]); t2 = ptile("t2", [128, 128])
            cdma(ef, ef.ap, CI["ef"]); cdma(eb, eb.ap, CI["eb"])
            cdma(dmf, dmf.ap, CI["dmf"]); cdma(dmb, dmb.ap, CI["dmb"])
            for h in range(H):
                kb.op("act", lambda e, h=h: e.activation(t1.ap, ef.ap, AF.Exp, scale=lg.ap[:, h:h + 1]), [ef, lg], [t1])
                kb.op("act", lambda e, h=h: e.activation(t2.ap, eb.ap, AF.Exp, scale=lg.ap[:, 4 + h:5 + h]), [eb, lg], [t2])
                kb.op("dve", lambda e: e.tensor_tensor(t1.ap, t1.ap, dmf.ap, ALU.mult), [t1, dmf], [t1])
                kb.op("dve", lambda e: e.tensor_tensor(t2.ap, t2.ap, dmb.ap, ALU.mult), [t2, dmb], [t2])
                kb.op("dve", lambda e, h=h: e.tensor_tensor(dmT.ap[:, h * 128:(h + 1) * 128], t1.ap, t2.ap, ALU.add),
                      [t1, t2], [dmT])
            dbg_out("ropec", ropec.ap, [128, ncol], [ropec])
            dbg_out("ropes", ropes.ap, [128, ncol], [ropes])
            dbg_out("wtab", wtab.ap[:, 0:28], [128, 28], [wtab])
            dbg_out("dmT", dmT.ap, [128, 512], [dmT])
            kb.barrier()

        with ExitStack() as ph:
            def ptile(name, shape, dt=F32):
                return T(sbt(ph, name, shape, dt)[:])
            gm = ptile("gm", [128, KT]); gf = ptile("gf", [128, KT]); gq = ptile("gq", [128, KT])
            cdma(gm, gm.ap, I["g_mix"].rearrange("(kt p) -> p kt", p=128), allow_slow_non_contiguous=True)
            cdma(gf, gf.ap, I["g_ffn"].rearrange("(kt p) -> p kt", p=128), allow_slow_non_contiguous=True)
            kb.op("dve", lambda e: e.tensor_scalar(gq.ap, gm.ap, DK ** -0.5, None, ALU.mult), [gm], [gq])
            NST = 4
            st32 = [ptile(f"st32_{i}", [128, 4096]) for i in range(NST)]
            st16 = [ptile(f"st16_{i}", [128, 4096], BF16) for i in range(NST)]
            cnt = [0]

            def cast_rows(i, kts, col0, ncols, width, scale_t):
                s3 = st32[i].ap.rearrange("p (kt c) -> p kt c", c=width)
                d3 = st16[i].ap.rearrange("p (kt c) -> p kt c", c=width)
                for kt in kts:
                    eng = ("act", "dve")[cnt[0] % 2]
                    cnt[0] += 1
                    src_ = s3[:, kt, col0:col0 + ncols]
                    dst_ = d3[:, kt, col0:col0 + ncols]
                    if scale_t is None:
                        if eng == "act":
                            kb.op("act", lambda e: e.activation(dst_, src_, AF.Copy), [st32[i]], [st16[i]])
                        else:
                            kb.op(eng, lambda e: e.tensor_copy(dst_, src_), [st32[i]], [st16[i]])
                    else:
                        if eng == "act":
                            kb.op("act", lambda e: e.activation(dst_, src_, AF.Copy, scale=scale_t.ap[:, kt:kt + 1]), [st32[i], scale_t], [st16[i]])
                        else:
                            kb.op(eng, lambda e: e.tensor_scalar(dst_, src_, scale_t.ap[:, kt:kt + 1], None, ALU.mult), [st32[i], scale_t], [st16[i]])

            blk = [0]

            def load(i, src, nkt, col0, ncols, width, dcol0):
                s3 = st32[i].ap.rearrange("p (kt c) -> p kt c", c=width)
                kb.dma("sp", s3[:, 0:nkt, dcol0:dcol0 + ncols],
                       src.rearrange("(kt p) c -> p kt c", p=128)[:, 0:nkt, col0:col0 + ncols], writes=[st32[i]], group=True)

            jobs = []

            def conv_block(dst_ap, parts, width=512, zero_tail=False):
                i = blk[0] % NST
                blk[0] += 1

                def do_load():
                    for (src, nkt, c0, ncols, d0, sc, r0) in parts:
                        if r0:
                            s3 = st32[i].ap.rearrange("p (kt c) -> p kt c", c=width)
                            kb.dma("sp", s3[:, 0:nkt, d0:d0 + ncols],
                                   src.rearrange("(kt p) c -> p kt c", p=128)[:, r0:r0 + nkt, c0:c0 + ncols], writes=[st32[i]], group=True)
                        else:
                            load(i, src, nkt, c0, ncols, width, d0)

                def do_rest():
                    for (src, nkt, c0, ncols, d0, sc, r0) in parts:
                        cast_rows(i, range(nkt), d0, ncols, width, sc)
                    if zero_tail:
                        nk = parts[0][1]
                        kb.op("dve", lambda e: e.memset(st16[i].ap[:, nk * width:], 0.0), [], [st16[i]])
                    kb.dma("sp", dst_ap, st16[i].ap, reads=[st16[i]])
                jobs.append((do_load, do_rest))

            WI = I["w_in"]
            conv_block(wS[0], [(WI, 8, OFF_U, 512, 0, gm, 0)])
            conv_block(wS[1], [(WI, 8, OFF_Q, 512, 0, gq, 0)])
            conv_block(wS[2], [(WI, 8, OFF_K, 512, 0, gm, 0)])
            for j in range(2):
                conv_block(wS[3 + j], [(WI, 8, OFF_V + 512 * j, 512, 0, gm, 0)])
                conv_block(wS[5 + j], [(WI, 8, OFF_GR + 512 * j, 512, 0, gm, 0)])
                conv_block(wS[7 + j], [(I["w_out"], 8, 512 * j, 512, 0, None, 0)])
            for b in range(11):
                conv_block(wS[9 + b], [(I["w_f1"], 8, 256 * b, 256, 0, gf, 0),
                                       (I["w_f1"], 8, DFF + 256 * b, 256, 256, gf, 0)])
            for kg in range(3):
                nk = min(8, FT - 8 * kg)
                for hf in range(2):
                    conv_block(wS[20 + 2 * kg + hf], [(I["w_f2"], nk, 512 * hf, 512, 0, None, 8 * kg)], zero_tail=(nk < 8))
            def merge_job(mt):
                i = blk[0] % NST
                blk[0] += 1
                a3 = st32[i].ap[:, 0:3072].rearrange("p (kt c) -> p kt c", c=384)
                g3 = st32[i].ap[:, 3072:4096].rearrange("p (kt c) -> p kt c", c=256)
                o3 = st16[i].ap[:, 0:3072].rearrange("p (kt c) -> p kt c", c=384)
                og3 = st16[i].ap[:, 3072:4096].rearrange("p (kt c) -> p kt c", c=256)

                def src3(src, c0):
                    return src.rearrange("(kt p) c -> p kt c", p=128)[:, :, c0:c0 + 128]

                def do_load():
                    kb.dma("sp", a3[:, :, 0:128], src3(I["w_ro"], mt * 128), writes=[st32[i]], group=True)
                    kb.dma("sp", a3[:, :, 128:256], src3(WI, OFF_GA + mt * 128), writes=[st32[i]], group=True)
                    kb.dma("sp", a3[:, :, 256:384], src3(WI, OFF_GB + mt * 128), writes=[st32[i]], group=True)
                    kb.dma("sp", g3[:, :, 0:128], src3(I["w_glu"], mt * 128), writes=[st32[i]], group=True)
                    kb.dma("sp", g3[:, :, 128:256], src3(I["w_glu"], 1024 + mt * 128), writes=[st32[i]], group=True)

                def do_rest():
                    for kt in range(8):
                        kb.op("act", lambda e: e.activation(o3[:, kt, 0:128], a3[:, kt, 0:128], AF.Copy), [st32[i]], [st16[i]])
                        kb.op("dve", lambda e: e.tensor_scalar(o3[:, kt, 128:384], a3[:, kt, 128:384], gm.ap[:, kt:kt + 1], None, ALU.mult),
                              [st32[i], gm], [st16[i]])
                    kb.op("act", lambda e: e.activation(og3, g3, AF.Copy), [st32[i]], [st16[i]])
                    kb.dma("sp", wM[mt], st16[i].ap, reads=[st16[i]])
                jobs.append((do_load, do_rest))

            for mt in range(8):
                merge_job(mt)
            LOOK = NST - 1
            for j in range(min(LOOK, len(jobs))):
                jobs[j][0]()
            for j in range(len(jobs)):
                if j + LOOK < len(jobs):
                    jobs[j + LOOK][0]()
                jobs[j][1]()
            kb.barrier()

        import types
        X = types.SimpleNamespace(**{k: v for k, v in locals().items() if k != "X"})
        build_s5_params(X)
        build_main(X)
        kb.finish()
    return nc, consts, DBG


def build_s5_params(X):
    nc, kb, I, CI, sbt, cdma, sincos = X.nc, X.kb, X.I, X.CI, X.sbt, X.cdma, X.sincos
    psum, ps_next, ident_f, misc = X.psum, X.ps_next, X.ident_f, X.misc
    nsb = X.nsb_max
    top = X.top
    X.rho = T(sbt(top, "rho", [128, 64], F32)[:])
    rho = X.rho
    with ExitStack() as ph:
        def ptile(name, shape, dt=F32):
            return T(sbt(ph, name, shape, dt)[:])
        kexp = ptile("kexp", [128, 16])
        sbidx = ptile("sbidx", [128, nsb])
        maskf = ptile("maskf", [128, 128]); maskb = ptile("maskb", [128, 128])
        cdma(kexp, kexp.ap, CI["kexp"]); cdma(sbidx, sbidx.ap, CI["sbidx"])
        cdma(maskf, maskf.ap, CI["maskf"]); cdma(maskb, maskb.ap, CI["maskb"])
        dtab = ptile("dtab", [128, NG])
        for i in range(NT):
            cdma(dtab, dtab.ap[i * GC:(i + 1) * GC, :], I["ssm_d"].rearrange("(g c) -> c g", c=GC),
                 allow_slow_non_contiguous=True)
        macc = ptile("macc", [128, NG * 128])
        phi = ptile("phi", [128, 64])
        lr = ptile("lr", [128, NG]); li = ptile("li", [128, NG]); dtb = ptile("dtb", [128, NG])
        xx = ptile("xx", [128, NG]); th = ptile("th", [128, NG])
        bre = ptile("bre", [128, NG * GC]); bim = ptile("bim", [128, NG * GC])
        cre = ptile("cre", [128, NG * GC]); cim = ptile("cim", [128, NG * GC])
        cin = ptile("cin", [128, 128])
        xk = ptile("xk", [128, NG * 16]); thk = ptile("thk", [128, NG * 16])
        tfk = ptile("tfk", [128, NG * 16]); tik = ptile("tik", [128, NG * 16], I32)
        AR = ptile("AR", [128, NG * 16]); AI = ptile("AI", [128, NG * 16])
        tabs = {n: ptile("tab_" + n, [128, NG * 16]) for n in
                ("s1nat", "s2nat", "s1swp", "s2swp", "s1far", "s2far", "s1fsw", "s2fsw")}
        t_a = ptile("t_a", [128, NG]); t_b = ptile("t_b", [128, NG]); t_c = ptile("t_c", [128, NG])
        coefr = ptile("coefr", [128, NG]); coefi = ptile("coefi", [128, NG])
        bbr = ptile("bbr", [128, NG * GC]); bbi = ptile("bbi", [128, NG * GC])
        tb1 = ptile("tb1", [128, NG * GC])
        Z = {n: ptile("Z_" + n, [128, NG * 128]) for n in ("wnat", "wswp", "v", "far", "fsw")}
        ztmp = ptile("ztmp", [128, NG * 128])
        stage = [ptile(f"s5stage{i}", [128, 512], BF16) for i in range(2)]
        rstage = [ptile(f"rstage{i}", [128, 2 * nsb]) for i in range(2)]
        rang = ptile("rang", [128, nsb]); rtf = ptile("rtf", [128, nsb]); rti = ptile("rti", [128, nsb], I32)
        mstage = T(Z["fsw"].ap.bitcast(BF16)[:, 0:NG * 128])

        def v3(t, w):
            return t.ap.rearrange("p (g k) -> p g k", k=w)

        for d in range(2):
            for half in range(2):
                sl = slice(half * 64, half * 64 + 64)
                cdma(lr, lr.ap[sl, :], I["lam_re"][d].rearrange("g p -> p g"), allow_slow_non_contiguous=True)
                cdma(li, li.ap[sl, :], I["lam_im"][d].rearrange("g p -> p g"), allow_slow_non_contiguous=True)
                cdma(bre, bre.ap[sl, :].rearrange("p (g c) -> p g c", c=GC), I["b_re"][d].rearrange("g p c -> p g c"))
                cdma(bim, bim.ap[sl, :].rearrange("p (g c) -> p g c", c=GC), I["b_im"][d].rearrange("g p c -> p g c"))
            cdma(dtb, dtb.ap, I["log_step"][d].partition_broadcast(128))
            for (src, dst) in ((I["c_re"], cre), (I["c_im"], cim)):
                flat = src[d].rearrange("g c p -> (g c) p")
                for j in range(4):
                    cdma(cin, cin.ap[:, 0:64], flat[j * 128:(j + 1) * 128, :])
                    cdma(cin, cin.ap[:, 64:128], flat[j * 128:(j + 1) * 128, :])
                    pt = ps_next()
                    kb.op("pe", lambda e: e.transpose(pt.ap[:, 0:128], cin.ap, ident_f.ap), [cin, ident_f], [pt])
                    kb.op("act", lambda e, j=j: e.activation(dst.ap[:, j * 128:(j + 1) * 128], pt.ap[:, 0:128], AF.Copy), [pt], [dst])
            kb.op("act", lambda e: e.activation(dtb.ap, dtb.ap, AF.Exp), [dtb], [dtb])
            kb.op("dve", lambda e: e.tensor_tensor(xx.ap, lr.ap, dtb.ap, ALU.mult), [lr, dtb], [xx])
            kb.op("dve", lambda e: e.tensor_tensor(th.ap, li.ap, dtb.ap, ALU.mult), [li, dtb], [th])
            kb.op("dve", lambda e: e.tensor_scalar(phi.ap[:, d * NG:(d + 1) * NG], th.ap, float(NT), None, ALU.mult), [th], [phi])
            kx = kexp.ap.unsqueeze(1).broadcast_to([128, NG, 16])
            kb.op("dve", lambda e: e.tensor_tensor(v3(xk, 16), xx.ap.unsqueeze(2).broadcast_to([128, NG, 16]), kx, ALU.mult),
                  [xx, kexp], [xk])
            kb.op("dve", lambda e: e.tensor_tensor(v3(thk, 16), th.ap.unsqueeze(2).broadcast_to([128, NG, 16]), kx, ALU.mult),
                  [th, kexp], [thk])
            kb.op("act", lambda e: e.activation(xk.ap, xk.ap, AF.Exp), [xk], [xk])
            kb.op("act", lambda e: e.activation(rho.ap[:, d * NG:(d + 1) * NG], v3(xk, 16)[:, :, 15], AF.Copy), [xk], [rho])
            sincos("dve", thk, NG * 16, AI, AR, tfk, tik)
            kb.op("dve", lambda e: e.tensor_tensor(AR.ap, AR.ap, xk.ap, ALU.mult), [AR, xk], [AR])
            kb.op("dve", lambda e: e.tensor_tensor(AI.ap, AI.ap, xk.ap, ALU.mult), [AI, xk], [AI])
            ar1 = v3(AR, 16)[:, :, 8]
            ai1 = v3(AI, 16)[:, :, 8]
            kb.op("dve", lambda e: e.tensor_scalar(t_a.ap, ar1, -1.0, None, ALU.add), [AR], [t_a])
            kb.op("dve", lambda e: e.tensor_tensor(t_b.ap, lr.ap, lr.ap, ALU.mult), [lr], [t_b])
            kb.op("dve", lambda e: e.tensor_tensor(t_c.ap, li.ap, li.ap, ALU.mult), [li], [t_c])
            kb.op("dve", lambda e: e.tensor_tensor(t_b.ap, t_b.ap, t_c.ap, ALU.add), [t_b, t_c], [t_b])
            kb.op("dve", lambda e: e.reciprocal(t_b.ap, t_b.ap), [t_b], [t_b])
            kb.op("dve", lambda e: e.tensor_tensor(coefr.ap, t_a.ap, lr.ap, ALU.mult), [t_a, lr], [coefr])
            kb.op("dve", lambda e: e.tensor_tensor(t_c.ap, ai1, li.ap, ALU.mult), [AI, li], [t_c])
            kb.op("dve", lambda e: e.tensor_tensor(coefr.ap, coefr.ap, t_c.ap, ALU.add), [coefr, t_c], [coefr])
            kb.op("dve", lambda e: e.tensor_tensor(coefr.ap, coefr.ap, t_b.ap, ALU.mult), [coefr, t_b], [coefr])
            kb.op("dve", lambda e: e.tensor_tensor(coefi.ap, ai1, lr.ap, ALU.mult), [AI, lr], [coefi])
            kb.op("dve", lambda e: e.tensor_tensor(t_c.ap, t_a.ap, li.ap, ALU.mult), [t_a, li], [t_c])
            kb.op("dve", lambda e: e.tensor_tensor(coefi.ap, coefi.ap, t_c.ap, ALU.subtract), [coefi, t_c], [coefi])
            kb.op("dve", lambda e: e.tensor_tensor(coefi.ap, coefi.ap, t_b.ap, ALU.mult), [coefi, t_b], [coefi])
            cr_b = coefr.ap.unsqueeze(2).broadcast_to([128, NG, GC])
            ci_b = coefi.ap.unsqueeze(2).broadcast_to([128, NG, GC])
            kb.op("dve", lambda e: e.tensor_tensor(v3(bbr, GC), v3(bre, GC), cr_b, ALU.mult), [bre, coefr], [bbr])
            kb.op("dve", lambda e: e.tensor_tensor(v3(tb1, GC), v3(bim, GC), ci_b, ALU.mult), [bim, coefi], [tb1])
            kb.op("dve", lambda e: e.tensor_tensor(bbr.ap, bbr.ap, tb1.ap, ALU.subtract), [bbr, tb1], [bbr])
            kb.op("dve", lambda e: e.tensor_tensor(v3(bbi, GC), v3(bim, GC), cr_b, ALU.mult), [bim, coefr], [bbi])
            kb.op("dve", lambda e: e.tensor_tensor(v3(tb1, GC), v3(bre, GC), ci_b, ALU.mult), [bre, coefi], [tb1])
            kb.op("dve", lambda e: e.tensor_tensor(bbi.ap, bbi.ap, tb1.ap, ALU.add), [bbi, tb1], [bbi])
            roles = {"s1nat": ((AR, 1), (AI, 1)), "s2nat": ((AI, -1), (AR, 1)),
                     "s1swp": ((AI, 1), (AR, 1)), "s2swp": ((AR, 1), (AI, -1)),
                     "s1far": ((AR, 1), (AI, -1)), "s2far": ((AI, -1), (AR, -1)),
                     "s1fsw": ((AI, -1), (AR, 1)), "s2fsw": ((AR, -1), (AI, -1))}
            for n, (lo, hi) in roles.items():
                for (src, sg), sl in ((lo, slice(0, 64)), (hi, slice(64, 128))):
                    kb.op("act", lambda e, src=src, sg=sg, sl=sl, n=n: e.activation(
                        tabs[n].ap[sl, :], src.ap[sl, :], AF.Copy, scale=float(sg)), [src], [tabs[n]])
            if d == 0:
                k_w, k_v, k_f = slice(14, 6, -1), slice(0, 8), slice(8, 16)
            else:
                k_w, k_v, k_f = slice(7, 15), slice(7, None, -1), slice(15, 7, -1)

            def zbuild(zn, xr, xi, s1, s2, ksl):
                z4 = Z[zn].ap.rearrange("p (g i c) -> p g i c", i=NT, c=GC)
                t4 = ztmp.ap.rearrange("p (g i c) -> p g i c", i=NT, c=GC)
                xr4 = v3(xr, GC).unsqueeze(2).broadcast_to([128, NG, NT, GC])
                xi4 = v3(xi, GC).unsqueeze(2).broadcast_to([128, NG, NT, GC])
                s14 = v3(tabs[s1], 16)[:, :, ksl].unsqueeze(3).broadcast_to([128, NG, NT, GC])
                s24 = v3(tabs[s2], 16)[:, :, ksl].unsqueeze(3).broadcast_to([128, NG, NT, GC])
                kb.op("dve", lambda e: e.tensor_tensor(z4, xr4, s14, ALU.mult), [xr, tabs[s1]], [Z[zn]])
                kb.op("dve", lambda e: e.tensor_tensor(t4, xi4, s24, ALU.mult), [xi, tabs[s2]], [ztmp])
                kb.op("dve", lambda e: e.tensor_tensor(Z[zn].ap, Z[zn].ap, ztmp.ap, ALU.add), [Z[zn], ztmp], [Z[zn]])

            zbuild("wnat", bbr, bbi, "s1nat", "s2nat", k_w)
            zbuild("wswp", bbr, bbi, "s1swp", "s2swp", k_w)
            zbuild("v", cre, cim, "s1far", "s2far", k_v)
            zbuild("far", cre, cim, "s1far", "s2far", k_f)
            zbuild("fsw", cre, cim, "s1fsw", "s2fsw", k_f)
            msk = maskf if d == 0 else maskb
            for g in range(NG):
                gs = slice(g * 128, (g + 1) * 128)
                st = stage[g % 2]
                p1 = ps_next()
                kb.op("pe", lambda e: e.transpose(p1.ap[:, 0:128], Z["wnat"].ap[:, gs], ident_f.ap), [Z["wnat"], ident_f], [p1])
                kb.op("pe", lambda e: e.transpose(p1.ap[:, 128:256], Z["wswp"].ap[:, gs], ident_f.ap), [Z["wswp"], ident_f], [p1])
                kb.op("pe", lambda e: e.matmul(p1.ap[:, 256:384], Z["wnat"].ap[:, gs], Z["v"].ap[:, gs], start=True, stop=True),
                      [Z["wnat"], Z["v"]], [p1])
                kb.op("act", lambda e: e.activation(st.ap[:, 0:256], p1.ap[:, 0:256], AF.Copy), [p1], [st])
                kb.op("act", lambda e: e.activation(st.ap[:, 256:384], Z["far"].ap[:, gs], AF.Copy), [Z["far"]], [st])
                kb.op("act", lambda e: e.activation(st.ap[:, 384:512], Z["fsw"].ap[:, gs], AF.Copy), [Z["fsw"]], [st])
                if d == 0:
                    kb.op("dve", lambda e: e.tensor_tensor(macc.ap[:, gs], p1.ap[:, 256:384], msk.ap, ALU.mult), [p1, msk], [macc])
                else:
                    kb.op("dve", lambda e: e.tensor_tensor(ztmp.ap[:, gs], p1.ap[:, 256:384], msk.ap, ALU.mult), [p1, msk], [ztmp])
                    kb.op("dve", lambda e: e.tensor_tensor(macc.ap[:, gs], macc.ap[:, gs], ztmp.ap[:, gs], ALU.add), [macc, ztmp], [macc])
                kb.dma("sp", X.s5w[d * NG + g], st.ap, reads=[st])
        for g in range(NG):
            gs = slice(g * 128, (g + 1) * 128)
            kb.op("dve", lambda e, g=g, gs=gs: e.scalar_tensor_tensor(macc.ap[:, gs], ident_f.ap, dtab.ap[:, g:g + 1], macc.ap[:, gs],
                                                                ALU.mult, ALU.add), [ident_f, dtab, macc], [macc])
        kb.op("act", lambda e: e.activation(mstage.ap, macc.ap, AF.Copy), [macc], [mstage, Z["fsw"]])
        kb.dma("sp", X.s5m.rearrange("g p c -> p g c"), mstage.ap.rearrange("p (g c) -> p g c", c=128), reads=[mstage])
        phi2 = ptile("phi2", [128, 64])
        sg2 = ptile("sg2", [128, 2])
        kb.op("dve", lambda e: e.tensor_scalar(phi2.ap, phi.ap, 1.0 / TWO_PI, None, ALU.mult), [phi], [phi2])
        kb.op("dve", lambda e: e.tensor_scalar(sg2.ap, misc.ap[:, 5:7], TWO_PI, None, ALU.mult), [misc], [sg2])
        kb.barrier()
        zw = Z["wswp"].ap
        assert nsb <= 600
        tsets = [(rang, rtf, rti, T(zw[:, 1800:1800 + nsb])),
                 (T(zw[:, 0:nsb]), T(zw[:, 600:600 + nsb]), T(zw.bitcast(I32)[:, 1200:1200 + nsb]), T(Z["v"].ap[:, 0:nsb]))]
        for item in range(64):
            d = item // NG
            rs = rstage[item % 2]
            r3 = rs.ap.rearrange("p (t s) -> p t s", t=2)
            rang_, rtf_, rti_, rsq_ = tsets[item % 2]
            a = rang_.ap
            kb.op("dve", lambda e: e.tensor_scalar(a, sbidx.ap, phi2.ap[:, item:item + 1], None, ALU.mult), [sbidx, phi2], [rang_])
            kb.op("dve", lambda e: e.tensor_copy(rti_.ap, a), [rang_], [rti_])
            kb.op("dve", lambda e: e.tensor_copy(rtf_.ap, rti_.ap), [rti_], [rtf_])
            kb.op("dve", lambda e: e.tensor_tensor(a, a, rtf_.ap, ALU.subtract), [rang_, rtf_], [rang_])
            kb.op("dve", lambda e: e.tensor_scalar(a, a, 0.4999999, -0.4999999, ALU.min, ALU.max), [rang_], [rang_])
            kb.op("act", lambda e: e.activation(r3[:, 1, :], a, AF.Sin, scale=sg2.ap[:, d:d + 1]), [rang_, sg2], [rs])
            kb.op("act", lambda e: e.activation(rsq_.ap, a, AF.Sin, scale=math.pi), [rang_], [rsq_])
            kb.op("act", lambda e: e.activation(rsq_.ap, rsq_.ap, AF.Square), [rsq_], [rsq_])
            kb.op("act", lambda e: e.activation(r3[:, 0, :], rsq_.ap, AF.Copy, scale=-2.0, bias=1.0), [rsq_], [rs])
            kb.dma("sp", X.rot[item].rearrange("p t s -> p (t s)"), rs.ap, reads=[rs])
        if "s5w" in X.dbg:
            pass
        kb.barrier()


def build_main(X):
    nc, kb, I, CI, sbt, cdma = X.nc, X.kb, X.I, X.CI, X.sbt, X.cdma
    psum, ident_f, ident_b, misc = X.psum, X.ident_f, X.ident_b, X.misc
    ropec, ropes, dmT, wtab, gfin, st0, umeta, rho = X.ropec, X.ropes, X.dmT, X.wtab, X.gfin, X.st0, X.umeta, X.rho
    wS, wM, s5w, s5m, rot = X.wS, X.wM, X.s5w, X.s5m, X.rot
    NP_, SP_, NS_, SS_ = X.NP_, X.SP_, X.NS_, X.SS_
    nch_max, nsb_max = X.nch_max, X.nsb_max
    top = X.top
    dbg = X.dbg
    seqs = [("p", i, SP_) for i in range(NP_)] + [("s", i, SS_) for i in range(NS_)]
    smax = max(s[2] for s in seqs)
    ngr_max = smax // 512
    nr_max = smax // NT

    uscrA = X.dscr("uscrA", [NG, GC, NT, nsb_max], BF16)
    uscr = X.dscr("uscr", [NG, NT, GC, nsb_max], BF16)
    yscr = X.dscr("yscr", [NG, 128, nr_max], BF16)
    yscrB = X.dscr("yscrB", [NG, GC, NT, nr_max], BF16)
    relay_u = T(None); relay_y = T(None)
    kscr = X.dscr("kscr", [nch_max, 128, 512], BF16)
    vscr = X.dscr("vscr", [nch_max, 128, H * DV], BF16)
    ntscr = X.dscr("ntscr", [ngr_max, 4, 128, 1024], BF16)
    ntst = [T(None) for _ in range(4)]
    qscr = X.dscr("qscr", [nch_max, 128, 512], BF16)
    sgscr = X.dscr("sgscr", [nch_max, 128, H * DV], BF16)
    kst = [T(None) for _ in range(4)]
    vst = [T(None) for _ in range(8)]
    stash = X.dscr("stash", [nch_max, 128, H * DV], BF16)

    def ctile(name, shape, dt=F32):
        return T(sbt(top, name, shape, dt)[:])

    def sub(t_ap):
        return T(t_ap)

    ring_t = sbt(top, "ring", [128, 4 * 4096], BF16)
    ring = [T(ring_t[:, i * 4096:(i + 1) * 4096]) for i in range(4)]
    ring_i = [0]

    def wload(src_ap, ncols=4096):
        t = ring[ring_i[0] % 4]
        ring_i[0] += 1
        kb.dma("sp", t.ap[:, 0:ncols], src_ap, writes=[t])
        return t

    xg_t = sbt(top, "xg", [128, 4 * D], F32)
    _xg0 = [T(xg_t[:, c * D:(c + 1) * D]) for c in range(4)]
    xg = [_xg0, _xg0]
    xs_t = sbt(top, "xs", [128, 2 * D], BF16)
    _xs2 = [T(xs_t[:, c * D:(c + 1) * D]) for c in range(2)]
    xs = [_xs2[0], _xs2[1], _xs2[0], _xs2[1]]
    nT_t = sbt(top, "nT", [128, KT * 512], BF16)
    nTq = [T(nT_t[:, b * 1024:(b + 1) * 1024]) for b in range(4)]
    nT3 = nT_t[:].rearrange("p (kt t) -> p kt t", t=512)
    ssq = [ctile(f"ssq{c}", [128, 1]) for c in range(4)]
    rsd = [ctile(f"rsd{c}", [128, 1]) for c in range(4)]
    stb = ctile("stb", [128, H * DV])
    stf = stb
    stf_bf = ctile("stf_bf", [128, H * DV], BF16)
    stb_bf = [ctile(f"stb_bf{i}", [128, H * DV], BF16) for i in range(2)]
    rt_t = ctile("rt_t", [128, 512]); rt_u1 = ctile("rt_u1", [128, 256]); rt_u2 = ctile("rt_u2", [128, 256])
    kr = [ctile(f"kr{c}", [128, 512], BF16) for c in range(4)]
    kw = [ctile(f"kw{c}", [128, 512], BF16) for c in range(4)]
    vv = [ctile(f"vv{c}", [128, H * DV], BF16) for c in range(4)]

    ps_rr = [0]

    def ps_next():
        b = ps_rr[0] % 8
        ps_rr[0] += 1
        return psum[b]

    def ps_pair():
        if ps_rr[0] % 2:
            ps_rr[0] += 1
        b = ps_rr[0] % 8
        ps_rr[0] += 2
        return psum[b], psum[b + 1]

    def bc4(ap2, h=4, two=2):
        return ap2.unsqueeze(1).unsqueeze(1).broadcast_to([128, h, two, 64])

    def bc3(ap2, h=4):
        return ap2.unsqueeze(1).broadcast_to([128, h, 64])

    def rope(ps, n, out_t, rows=128):
        cs = ropec.ap[0:rows, n * 64:(n + 1) * 64]
        sn = ropes.ap[0:rows, n * 64:(n + 1) * 64]
        x4 = ps.ap[0:rows, :].rearrange("p (h two f) -> p h two f", h=4, two=2)
        t4 = rt_t.ap[0:rows, :].rearrange("p (h two f) -> p h two f", h=4, two=2)
        o4 = out_t.ap[0:rows, :].rearrange("p (h two f) -> p h two f", h=4, two=2)
        u1 = rt_u1.ap[0:rows, :].rearrange("p (h f) -> p h f", h=4)
        u2 = rt_u2.ap[0:rows, :].rearrange("p (h f) -> p h f", h=4)
        cs4 = cs.unsqueeze(1).unsqueeze(1).broadcast_to([rows, 4, 2, 64])
        sn3 = sn.unsqueeze(1).broadcast_to([rows, 4, 64])
        kb.op("dve", lambda e: e.tensor_tensor(t4, x4, cs4, ALU.mult), [ps, ropec], [rt_t])
        kb.op("dve", lambda e: e.tensor_tensor(u1, x4[:, :, 1, :], sn3, ALU.mult), [ps, ropes], [rt_u1])
        kb.op("dve", lambda e: e.tensor_tensor(u2, x4[:, :, 0, :], sn3, ALU.mult), [ps, ropes], [rt_u2])
        kb.op("pool", lambda e: e.tensor_tensor(o4[:, :, 0, :], t4[:, :, 0, :], u1, ALU.subtract), [rt_t, rt_u1], [out_t])
        kb.op("pool", lambda e: e.tensor_tensor(o4[:, :, 1, :], t4[:, :, 1, :], u2, ALU.add), [rt_t, rt_u2, out_t], [out_t])

    def rmsnorm_to_nT(src, nchunks=4, rows=128):
        banks = [ps_next() for _ in range(4)]
        for c in range(nchunks):
            kb.op("act", lambda e, c=c: e.activation(xs[c].ap[0:rows, :], src[c].ap[0:rows, :], AF.Square, accum_out=ssq[c].ap[0:rows, :]),
                  [src[c]], [xs[c], ssq[c]])
            kb.op("act", lambda e, c=c: e.activation(rsd[c].ap[0:rows, :], ssq[c].ap[0:rows, :], AF.Sqrt, bias=EPS, scale=1.0 / D), [ssq[c]], [rsd[c]])
            kb.op("dve", lambda e, c=c: e.reciprocal(rsd[c].ap[0:rows, :], rsd[c].ap[0:rows, :]), [rsd[c]], [rsd[c]])
            kb.op("dve", lambda e, c=c: e.tensor_scalar(xs[c].ap[0:rows, :], src[c].ap[0:rows, :], rsd[c].ap[0:rows, 0:1], None, ALU.mult),
                  [src[c], rsd[c]], [xs[c]])

            def tr(e, c=c):
                last = None
                for kt in range(KT):
                    pb = banks[kt // 2].ap.bitcast(BF16)
                    col = (kt % 2) * 512 + c * rows
                    last = e.transpose(pb[:, col:col + rows], xs[c].ap[0:rows, kt * 128:(kt + 1) * 128], ident_b.ap[0:rows, 0:rows])
                return last
            kb.op("pe", tr, [xs[c], ident_b], banks)
        ncol = nchunks * rows
        for b in range(4):
            pb = banks[b].ap.bitcast(BF16).rearrange("p (k t) -> p k t", t=512)
            dst = nTq[b].ap.rearrange("p (k t) -> p k t", t=512)
            kb.op("act" if b % 2 == 0 else "dve",
                  (lambda e, pb=pb, dst=dst: e.activation(dst[:, :, 0:ncol], pb[:, :, 0:ncol], AF.Copy)) if b % 2 == 0 else
                  (lambda e, pb=pb, dst=dst: e.tensor_copy(dst[:, :, 0:ncol], pb[:, :, 0:ncol])),
                  [banks[b]], [nTq[b]])

    def mm_tok(bank, slot, c, rows=128, ncols=512):
        s3 = slot.ap.rearrange("p (kt n) -> p kt n", n=512)

        def f(e):
            last = None
            for kt in range(KT):
                last = e.matmul(bank.ap[0:rows, 0:ncols], nT3[:, kt, c * rows:(c + 1) * rows], s3[:, kt, 0:ncols],
                                start=(kt == 0), stop=(kt == KT - 1))
            return last
        kb.op("pe", f, [slot] + nTq, [bank])

    def mm_feat(bank, slot, col0, ntok=512, width=512, nk=KT, rhs=None, rhs_tiles=None):
        s3 = slot.ap.rearrange("p (kt n) -> p kt n", n=width) if width else None

        def f(e):
            last = None
            for kt in range(nk):
                r = nT3[:, kt, 0:ntok] if rhs is None else rhs(kt)
                last = e.matmul(bank.ap[:, 0:ntok], s3[:, kt, col0:col0 + 128], r, start=(kt == 0), stop=(kt == nk - 1))
            return last
        kb.op("pe", f, [slot] + (nTq if rhs_tiles is None else rhs_tiles), [bank])

    mrow = NM
    cdma(xg[0][0], xg[0][0].ap[0:NM, :], I["meta"])
    rmsnorm_to_nT([xg[0][0]], nchunks=1, rows=NM)
    slot = wload(wS[0])
    for m in range(4):
        bk = ps_next()
        mm_feat(bk, slot, m * 128, ntok=NM)
        kb.op("act", lambda e, m=m, bk=bk: e.activation(umeta.ap[:, m * NM:(m + 1) * NM], bk.ap[:, 0:NM], AF.Copy), [bk], [umeta])
    slot = wload(wS[2])
    bk = ps_next()
    mm_tok(bk, slot, 0, rows=NM)
    rope(bk, nch_max, kr[0], rows=NM)
    kb.op("pool", lambda e: e.tensor_tensor(kw[0].ap[0:NM, :].rearrange("p (h d) -> p h d", h=4),
                                             kr[0].ap[0:NM, :].rearrange("p (h d) -> p h d", h=4),
                                             wtab.ap[0:NM, 24:28].unsqueeze(2).broadcast_to([NM, 4, 128]), ALU.mult),
          [kr[0], wtab], [kw[0]])
    for j in range(2):
        slot = wload(wS[3 + j])
        bk = ps_next()
        mm_tok(bk, slot, 0, rows=NM)
        kb.op("act", lambda e, j=j, bk=bk: e.activation(vv[0].ap[0:NM, j * 512:(j + 1) * 512], bk.ap[0:NM, :], AF.Copy), [bk], [vv[0]])
    b0, b1 = ps_pair()
    for h in range(H):
        bk = b0 if h < 2 else b1
        kb.op("pe", lambda e, h=h, bk=bk: e.matmul(bk.ap[:, (h % 2) * 256:(h % 2 + 1) * 256], kw[0].ap[0:NM, h * 128:(h + 1) * 128],
                                                 vv[0].ap[0:NM, h * 256:(h + 1) * 256], start=True, stop=True), [kw[0], vv[0]], [bk])
    kb.op("act", lambda e: e.activation(st0.ap[:, 0:512], b0.ap, AF.Copy), [b0], [st0])
    kb.op("act", lambda e: e.activation(st0.ap[:, 512:1024], b1.ap, AF.Copy), [b1], [st0])
    st0d = X.dscr("st0d", [128, H * DV], F32)
    kb.dma("pool", st0d, st0.ap, reads=[st0])
    X.dbg_out("umeta", umeta.ap, [128, 4 * NM], [umeta], BF16)
    X.dbg_out("st0", st0.ap, [128, H * DV], [st0])
    kb.barrier()


    def s5_stage(S, nsb, nr):
        with ExitStack() as ph:
            def ptile(name, shape, dt=F32):
                return T(sbt(ph, name, shape, dt)[:], key="s5_" + name)
            mall = ptile("mall", [128, NG * 128], BF16)
            kb.dma("sp", mall.ap.rearrange("p (g c) -> p g c", c=128), s5m.rearrange("g p c -> p g c"), writes=[mall])
            ug = [ptile(f"ug{i}", [128, nsb], BF16) for i in range(2)]
            wi = [ptile(f"wi{i}", [128, 512], BF16) for i in range(4)]
            rt = [ptile(f"rt{i}", [128, 2 * nsb_max]) for i in range(4)]
            t1s = [ptile(f"s5t1_{i}", [128, nsb]) for i in range(2)]; t2s = [ptile(f"s5t2_{i}", [128, nsb]) for i in range(2)]
            zins = [ptile(f"zin{i}", [128, nsb]) for i in range(2)]; zzs = [ptile(f"zz{i}", [128, nsb]) for i in range(2)]
            zc = [ptile(f"zc{i}", [128, nsb], BF16) for i in range(2)]
            zs = [ptile(f"zs{i}", [128, nsb], BF16) for i in range(2)]
            yst = [ptile(f"yst{i}", [128, nr], BF16) for i in range(2)]
            u2T = T(None)
            for g in range(NG):
                kb.dma("sp", uscr[g, :, :, 0:nsb], uscrA[g, :, :, 0:nsb].rearrange("c i s -> i c s"), writes=[u2T], owner=relay_u, group=True)
            yAT = [T(None) for _ in range(NG)]
            yBT = T(None)
            items = [(g, d) for g in range(NG) for d in (1, 0)]

            def bufs(ic):
                g, d = items[ic]
                p = ic % 2
                return dict(g=g, d=d, item=d * NG + g, u=ug[g % 2], pY=psum[g % 2], w=wi[ic % 4], r=rt[ic % 4], t1=t1s[p], t2=t2s[p], zin=zins[p], zz=zzs[p],
                            c_t=zc[p], s_t=zs[p], pX=psum[2 + 3 * p], pYy=psum[3 + 3 * p], pM=psum[4 + 3 * p])

            def P_pe(ic):
                b = bufs(ic)
                g, d, item, u, pY, w, r = b["g"], b["d"], b["item"], b["u"], b["pY"], b["w"], b["r"]
                pX, pYy, pM = b["pX"], b["pYy"], b["pM"]
                if d == 1:
                    kb.dma("sp", u.ap, uscr[g, :, :, 0:nsb].rearrange("i c s -> (i c) s"), reads=[u2T], writes=[u])
                    kb.op("pe", lambda e: e.matmul(pY.ap[:, 0:nr], mall.ap[:, g * 128:(g + 1) * 128], u.ap[:, 2:nsb], start=True, stop=False), [mall, u], [pY])
                kb.dma("sp", w.ap, s5w[item], writes=[w])
                kb.dma("sp", r.ap.rearrange("p (t s) -> p t s", t=2)[:, :, 0:nsb], rot[item][:, :, 0:nsb], writes=[r])
                kb.op("pe", lambda e: e.matmul(pX.ap[:, 0:nr], w.ap[:, 0:128], u.ap[:, 2:nsb], start=True, stop=True), [w, u], [pX])
                kb.op("pe", lambda e: e.matmul(pYy.ap[:, 0:nr], w.ap[:, 128:256], u.ap[:, 2:nsb], start=True, stop=True), [w, u], [pYy])
                if d == 0:
                    def fm(e):
                        e.matmul(pM.ap[:, 0:2], w.ap[:, 0:128], u.ap[:, 0:2], start=True, stop=True)
                        return e.matmul(pM.ap[:, 2:4], w.ap[:, 128:256], u.ap[:, 0:2], start=True, stop=True)
                    kb.op("pe", fm, [w, u], [pM])

            def P_dve(ic):
                b = bufs(ic)
                d, r = b["d"], b["r"]
                t1, t2, zin, pX, pYy, pM = b["t1"], b["t2"], b["zin"], b["pX"], b["pYy"], b["pM"]
                cosT = r.ap[:, 0:nsb]
                sinT = r.ap[:, nsb_max:nsb_max + nsb]
                if d == 0:
                    kb.op("dve", lambda e: e.tensor_tensor(t1.ap[:, 0:2], pM.ap[:, 0:2], cosT[:, 0:2], ALU.mult), [pM, r], [t1])
                    kb.op("dve", lambda e: e.tensor_tensor(t2.ap[:, 0:2], pM.ap[:, 2:4], sinT[:, 0:2], ALU.mult), [pM, r], [t2])
                kb.op("dve", lambda e: e.tensor_tensor(t1.ap[:, 2:nsb], pX.ap[:, 0:nr], cosT[:, 2:nsb], ALU.mult), [pX, r, t1], [t1])
                kb.op("dve", lambda e: e.tensor_tensor(t2.ap[:, 2:nsb], pYy.ap[:, 0:nr], sinT[:, 2:nsb], ALU.mult), [pYy, r, t2], [t2])
                lo = 0 if d == 0 else 2
                kb.op("dve", lambda e: e.tensor_tensor(zin.ap[:, lo:nsb], t1.ap[:, lo:nsb], t2.ap[:, lo:nsb], ALU.add), [t1, t2], [zin])

            def Q_ew(ic):
                b = bufs(ic)
                d, item, r = b["d"], b["item"], b["r"]
                zin, zz, c_t, s_t = b["zin"], b["zz"], b["c_t"], b["s_t"]
                cosT = r.ap[:, 0:nsb]
                sinT = r.ap[:, nsb_max:nsb_max + nsb]
                lo = 0 if d == 0 else 2
                coef = rho.ap[:, item:item + 1].broadcast_to([128, nsb - lo])
                if d == 0:
                    kb.op("dve", lambda e: e.tensor_tensor_scan(zz.ap[:, 0:nsb], coef, zin.ap[:, 0:nsb], 0.0, ALU.mult, ALU.add), [zin, rho], [zz])
                else:
                    kb.op("dve", lambda e: e.tensor_tensor_scan(zz.ap[:, 2:nsb][:, ::-1], coef, zin.ap[:, 2:nsb][:, ::-1], 0.0, ALU.mult, ALU.add),
                          [zin, rho], [zz])
                kb.op("pool", lambda e: e.tensor_tensor(s_t.ap[:, lo:nsb], zz.ap[:, lo:nsb], sinT[:, lo:nsb], ALU.mult), [zz, r], [s_t])
                kb.op("pool", lambda e: e.tensor_tensor(c_t.ap[:, lo:nsb], zz.ap[:, lo:nsb], cosT[:, lo:nsb], ALU.mult), [zz, r], [c_t])

            def Q_pe(ic):
                b = bufs(ic)
                g, d, pY, w = b["g"], b["d"], b["pY"], b["w"]
                c_t, s_t = b["c_t"], b["s_t"]
                if d == 0:
                    def ff(e):
                        e.matmul(pY.ap[:, 0:nr], w.ap[:, 256:384], c_t.ap[:, 1:nsb - 1], start=False, stop=False)
                        return e.matmul(pY.ap[:, 0:nr], w.ap[:, 384:512], s_t.ap[:, 1:nsb - 1], start=False, stop=True)
                    kb.op("pe", ff, [w, c_t, s_t], [pY])
                    ys_ = yst[g % 2]
                    kb.op("act", lambda e: e.activation(ys_.ap, pY.ap[:, 0:nr], AF.Gelu), [pY], [ys_])
                    kb.dma("act", yscr[g, :, 0:nr], ys_.ap, reads=[ys_], writes=[yAT[g]], owner=ys_)
                else:
                    def fb(e):
                        e.matmul(pY.ap[:, 0:nr - 1], w.ap[:, 256:384], c_t.ap[:, 3:nsb], start=False, stop=False)
                        return e.matmul(pY.ap[:, 0:nr - 1], w.ap[:, 384:512], s_t.ap[:, 3:nsb], start=False, stop=False)
                    kb.op("pe", fb, [w, c_t, s_t], [pY])

            P_pe(0)
            P_dve(0)
            for ic in range(len(items)):
                if ic + 1 < len(items):
                    P_pe(ic + 1)
                Q_ew(ic)
                if ic + 1 < len(items):
                    P_dve(ic + 1)
                Q_pe(ic)
            for g in range(NG):
                kb.dma("sp", yscrB[g, :, :, 0:nr], yscr[g, :, 0:nr].rearrange("(i c) s -> c i s", c=GC), reads=[yAT[g]], writes=[yBT], owner=relay_y, group=True)
            kb.barrier()


    def pass_b(xin, yout, S, ngr):
        with ExitStack() as ph:
            def ptile(name, shape, dt=F32):
                return T(sbt(ph, name, shape, dt)[:], key="pb_" + name)
            qr = [ptile(f"qr{c}", [128, 512], BF16) for c in range(4)]
            _sg2 = [ptile(f"sg{c}", [128, H * DV], BF16) for c in range(2)]
            sg = [_sg2[0], _sg2[1], _sg2[0], _sg2[1]]
            qf = [ptile(f"qf{i}", [128, 512], BF16) for i in range(2)]
            qb = [ptile(f"qb{i}", [128, 512], BF16) for i in range(2)]
            kf = kw
            qkT = [ptile(f"qkT{i}", [128, 1024], BF16) for i in range(2)]
            qfbT = [ptile(f"qfbT{i}", [128, 1024], BF16) for i in range(2)]
            stf_bfs = [stf_bf, ptile("stf_bf1", [128, H * DV], BF16)]
            PT = [ptile(f"PT{i}", [128, 512], BF16) for i in range(2)]
            og = [ptile(f"og{i}", [128, H * DV], BF16) for i in range(2)]
            ssh_t = [sbt(ph, f"ssh{i}", [128, 4], F32) for i in range(2)]
            ssh = [[T(ssh_t[i][:, h:h + 1]) for h in range(H)] for i in range(2)]
            rsh = [ptile(f"rsh{i}", [128, 4]) for i in range(2)]
            ogT_t = sbt(ph, "ogT", [128, KT * 512], BF16)
            ogT3 = ogT_t[:].rearrange("p (kt t) -> p kt t", t=512)
            ogTc = [T(ogT3[:, :, c * 128:(c + 1) * 128]) for c in range(4)]
            mixT = [ptile(f"mixT{m}", [128, 512], BF16) for m in range(8)]
            sga = [ptile("sga", [128, 512], BF16)] * 2
            sgb = [ptile("sgb", [128, 512], BF16)] * 2
            sag = [ptile("sag", [128, 512], BF16)] * 2
            ta = [ptile("ta", [128, 512], BF16)] * 2
            tb = [ptile("tb", [128, 512], BF16)] * 2
            actT = [ptile(f"actT{j}", [128, 512], BF16) for j in range(FT)]
            sgt = [ptile(f"sgt{i}", [128, 512], BF16) for i in range(2)]
            ytl = [ptile("ytl", [128, 4 * 512], BF16)] * 2

            kb.dma("pool", stf.ap, st0d, writes=[stf])
            kb.op("act", lambda e: e.activation(stf_bf.ap, stf.ap, AF.Copy), [stf], [stf_bf])
            xbufB = [xg[0], [st0] + [ptile(f"x2b{c}", [128, D]) for c in range(3)]]

            def load_x(gi):
                xb_ = xbufB[gi % 2]
                for c in range(4):
                    kb.dma("pool", xb_[c].ap, xin[gi * 512 + c * 128: gi * 512 + (c + 1) * 128, :], writes=[xb_[c]])

            xst = [[T(None, key=f"pb_xst{b}_{c}") for c in range(4)] for b in range(2)]
            load_x(0)
            for gi in range(ngr):
                xb = xbufB[gi % 2]
                yt = ytl[gi % 2]
                for kt in range(4):
                    kb.dma("sp", yt.ap[:, kt * 512:(kt + 1) * 512].rearrange("p (i s) -> p i s", i=NT),
                           yscrB[kt * 8:(kt + 1) * 8, :, :, gi * 64:(gi + 1) * 64].rearrange("g c i s -> (g c) i s"),
                           writes=[yt], group=True)
                if gi == 0:
                    for b in range(4):
                        kb.dma("pool", nTq[b].ap, ntscr[gi, b], writes=[nTq[b]])
                for c in range(4):
                    kb.dma("sp", qr[c].ap, qscr[gi * 4 + c], writes=[qr[c]])
                for c in range(4):
                    kb.dma("sp", kr[c].ap, kscr[gi * 4 + c], writes=[kr[c]])
                    kb.dma("sp", vv[c].ap, vscr[gi * 4 + c], writes=[vv[c]])
                def st_A(c):
                    n = gi * 4 + c
                    p2 = c % 2
                    sbb = stb_bf[p2]
                    kb.dma("pool", sbb.ap, stash[n], writes=[sbb])
                    kb.dma("pool", sg[c].ap, sgscr[n], writes=[sg[c]])

                    def sc3(src, dst, col):
                        kb.op("dve", lambda e: e.tensor_tensor(dst.ap.rearrange("p (h d) -> p h d", h=4), src.ap.rearrange("p (h d) -> p h d", h=4),
                                                                 wtab.ap[:, col:col + 4].unsqueeze(2).broadcast_to([128, 4, 128]), ALU.mult),
                              [src, wtab], [dst])
                    sc3(qr[c], qf[p2], 0)
                    sc3(qr[c], qb[p2], 8)
                    sc3(kr[c], kf[p2], 4)
                    bA = ps_next(); bB = ps_next()

                    def trq(e):
                        last = None
                        pa = bA.ap.bitcast(BF16); pb_ = bB.ap.bitcast(BF16)
                        for h in range(H):
                            hs = slice(h * 128, (h + 1) * 128)
                            e.transpose(pa[:, h * 128:(h + 1) * 128], qr[c].ap[:, hs], ident_b.ap)
                            e.transpose(pa[:, 512 + h * 128:512 + (h + 1) * 128], kr[c].ap[:, hs], ident_b.ap)
                            e.transpose(pb_[:, h * 128:(h + 1) * 128], qf[p2].ap[:, hs], ident_b.ap)
                            last = e.transpose(pb_[:, 512 + h * 128:512 + (h + 1) * 128], qb[p2].ap[:, hs], ident_b.ap)
                        return last
                    kb.op("pe", trq, [qr[c], kr[c], qf[p2], qb[p2], ident_b], [bA, bB])
                    kb.op("act", lambda e: e.activation(qkT[p2].ap, bA.ap.bitcast(BF16), AF.Copy), [bA], [qkT[p2]])
                    kb.op("dve", lambda e: e.tensor_copy(qfbT[p2].ap, bB.ap.bitcast(BF16)), [bB], [qfbT[p2]])

                def st_B(c):
                    p2 = c % 2
                    bS = ps_next()

                    def scores(e):
                        last = None
                        for h in range(H):
                            hs = slice(h * 128, (h + 1) * 128)
                            last = e.matmul(bS.ap[:, hs], qkT[p2].ap[:, 512 + h * 128:512 + (h + 1) * 128], qkT[p2].ap[:, hs], start=True, stop=True)
                        return last
                    kb.op("pe", scores, [qkT[p2]], [bS])
                    kb.op("dve", lambda e: e.tensor_tensor(PT[p2].ap, bS.ap, dmT.ap, ALU.mult), [bS, dmT], [PT[p2]])

                def st_G(c):
                    for j in range(2):
                        bk = ps_next()
                        mm_tok(bk, slot_g[j], c)
                        kb.op("act", lambda e: e.activation(sg[c].ap[:, j * 512:(j + 1) * 512], bk.ap, AF.Silu), [bk], [sg[c]])

                def st_K(c):
                    p2 = c % 2
                    k0, k1 = ps_pair()

                    def kvmm(e):
                        last = None
                        for h in range(H):
                            reg = (k0 if h < 2 else k1).ap[:, (h % 2) * 256:(h % 2 + 1) * 256]
                            last = e.matmul(reg, kf[p2].ap[:, h * 128:(h + 1) * 128], vv[c].ap[:, h * 256:(h + 1) * 256], start=True, stop=True)
                        return last
                    kb.op("pe", kvmm, [kf[p2], vv[c]], [k0, k1])
                    for h in range(H):
                        bk = k0 if h < 2 else k1
                        kb.op("dve", lambda e: e.scalar_tensor_tensor(
                            stf.ap[:, h * 256:(h + 1) * 256], stf.ap[:, h * 256:(h + 1) * 256], wtab.ap[:, 16 + h:17 + h],
                            bk.ap[:, (h % 2) * 256:(h % 2 + 1) * 256], ALU.mult, ALU.add), [stf, wtab, bk], [stf])
                    nxt = stf_bfs[(gi * 4 + c + 1) % 2]
                    kb.op("act", lambda e: e.activation(nxt.ap, stf.ap, AF.Copy), [stf], [nxt])

                def st_O(c):
                    p2 = c % 2
                    sbb = stb_bf[p2]
                    cur = stf_bfs[(gi * 4 + c) % 2]
                    o0, o1 = ps_pair()

                    def omm(e):
                        last = None
                        for h in range(H):
                            reg = (o0 if h < 2 else o1).ap[:, (h % 2) * 256:(h % 2 + 1) * 256]
                            hs = slice(h * 128, (h + 1) * 128)
                            vs = slice(h * 256, (h + 1) * 256)
                            e.matmul(reg, PT[p2].ap[:, hs], vv[c].ap[:, vs], start=True, stop=False)
                            e.matmul(reg, qfbT[p2].ap[:, hs], cur.ap[:, vs], start=False, stop=False)
                            last = e.matmul(reg, qfbT[p2].ap[:, 512 + h * 128:512 + (h + 1) * 128], sbb.ap[:, vs], start=False, stop=True)
                        return last
                    kb.op("pe", omm, [PT[p2], vv[c], qfbT[p2], cur, sbb], [o0, o1])
                    for h in range(H):
                        ob = o0 if h < 2 else o1
                        kb.op("act", lambda e: e.activation(og[p2].ap[:, h * 256:(h + 1) * 256], ob.ap[:, (h % 2) * 256:(h % 2 + 1) * 256],
                                                            AF.Square, accum_out=ssh[p2][h].ap), [ob], [og[p2], ssh[p2][h]])
                    kb.op("act", lambda e: e.activation(rsh[p2].ap, ssh_t[p2][:], AF.Sqrt, bias=EPS, scale=1.0 / DV), ssh[p2], [rsh[p2]])
                    kb.op("dve", lambda e: e.reciprocal(rsh[p2].ap, rsh[p2].ap), [rsh[p2]], [rsh[p2]])
                    for h in range(H):
                        ob = o0 if h < 2 else o1
                        kb.op("dve", lambda e: e.scalar_tensor_tensor(
                            og[p2].ap[:, h * 256:(h + 1) * 256], ob.ap[:, (h % 2) * 256:(h % 2 + 1) * 256], rsh[p2].ap[:, h:h + 1],
                            sg[c].ap[:, h * 256:(h + 1) * 256], ALU.mult, ALU.mult), [ob, rsh[p2], sg[c]], [og[p2]])

                def st_R(c):
                    p2 = c % 2
                    bT = ps_next()

                    def trog(e):
                        last = None
                        pt_ = bT.ap.bitcast(BF16)
                        for ft in range(KT):
                            last = e.transpose(pt_[:, ft * 128:(ft + 1) * 128], og[p2].ap[:, ft * 128:(ft + 1) * 128], ident_b.ap)
                        return last
                    kb.op("pe", trog, [og[p2], ident_b], [bT])
                    kb.op("act", lambda e: e.activation(ogTc[c].ap, bT.ap.bitcast(BF16).rearrange("p (kt t) -> p kt t", t=128), AF.Copy),
                          [bT], [ogTc[c]])

                st_A(0)
                for c in range(4):
                    if c + 1 < 4:
                        st_A(c + 1)
                    st_B(c)
                    st_K(c)
                    if c >= 1:
                        st_R(c - 1)
                    st_O(c)
                st_R(3)
                if gi + 1 < ngr:
                    load_x(gi + 1)
                for mt in range(8):
                    slot = wload(wM[mt])
                    a3 = slot.ap[:, 0:3072].rearrange("p (kt c) -> p kt c", c=384)
                    g3 = slot.ap[:, 3072:4096].rearrange("p (kt c) -> p kt c", c=256)
                    m2 = mt % 2
                    bbk, gak, gbk, avk, agk = ps_next(), ps_next(), ps_next(), ps_next(), ps_next()

                    def mm8(e, bank, c0, rhs):
                        last = None
                        for kt in range(KT):
                            last = e.matmul(bank.ap, a3[:, kt, c0:c0 + 128], rhs(kt), start=(kt == 0), stop=(kt == KT - 1))
                        return last

                    def mm4(e, bank, c0):
                        last = None
                        for kt in range(4):
                            last = e.matmul(bank.ap, g3[:, kt, c0:c0 + 128], yt.ap[:, kt * 512:(kt + 1) * 512], start=(kt == 0), stop=(kt == 3))
                        return last
                    kb.op("pe", lambda e: mm8(e, bbk, 0, lambda kt: ogT3[:, kt, :]), [slot] + ogTc, [bbk])
                    kb.op("pe", lambda e: mm8(e, gak, 128, lambda kt: nT3[:, kt, :]), [slot] + nTq, [gak])
                    kb.op("pe", lambda e: mm8(e, gbk, 256, lambda kt: nT3[:, kt, :]), [slot] + nTq, [gbk])
                    kb.op("pe", lambda e: mm4(e, avk, 0), [slot, yt], [avk])
                    kb.op("pe", lambda e: mm4(e, agk, 128), [slot, yt], [agk])
                    kb.op("act", lambda e: e.activation(sga[m2].ap, gak.ap, AF.Sigmoid), [gak], [sga[m2]])
                    kb.op("act", lambda e: e.activation(sgb[m2].ap, gbk.ap, AF.Sigmoid), [gbk], [sgb[m2]])
                    kb.op("act", lambda e: e.activation(sag[m2].ap, agk.ap, AF.Sigmoid), [agk], [sag[m2]])
                    kb.op("dve", lambda e: e.tensor_tensor(ta[m2].ap.rearrange("p (s i) -> p i s", i=NT),
                                                           avk.ap.rearrange("p (i s) -> p i s", i=NT),
                                                           sag[m2].ap.rearrange("p (i s) -> p i s", i=NT), ALU.mult), [avk, sag[m2]], [ta[m2]])
                    kb.op("pool", lambda e: e.tensor_tensor(ta[m2].ap, ta[m2].ap, sga[m2].ap, ALU.mult), [ta[m2], sga[m2]], [ta[m2]])
                    kb.op("dve", lambda e: e.tensor_tensor(tb[m2].ap, bbk.ap, sgb[m2].ap, ALU.mult), [bbk, sgb[m2]], [tb[m2]])
                    kb.op("pool", lambda e: e.tensor_tensor(mixT[mt].ap, ta[m2].ap, tb[m2].ap, ALU.add), [ta[m2], tb[m2]], [mixT[mt]])
                so = [wload(wS[7]), wload(wS[8])]
                for c in range(4):
                    b0, b1 = ps_pair()
                    for hf, bk in ((0, b0), (1, b1)):
                        s3 = so[hf].ap.rearrange("p (kt n) -> p kt n", n=512)

                        def wo(e, bk=bk, s3=s3, c=c):
                            last = None
                            for kt in range(KT):
                                last = e.matmul(bk.ap, mixT[kt].ap[:, c * 128:(c + 1) * 128], s3[:, kt, :], start=(kt == 0), stop=(kt == KT - 1))
                            return last
                        kb.op("pe", wo, [so[hf]] + mixT, [bk])
                        kb.op("dve", lambda e, bk=bk, hf=hf, c=c: e.tensor_tensor(xb[c].ap[:, hf * 512:(hf + 1) * 512], xb[c].ap[:, hf * 512:(hf + 1) * 512],
                                                                               bk.ap, ALU.add), [xb[c], bk], [xb[c]])
                rmsnorm_to_nT(xb)
                for b in range(11):
                    slot = wload(wS[9 + b])
                    for j in range(2):
                        gk, uk = ps_next(), ps_next()
                        mm_feat(gk, slot, j * 128)
                        mm_feat(uk, slot, 256 + j * 128)
                        st_ = sgt[j]
                        kb.op("act", lambda e, gk=gk, st_=st_: e.activation(st_.ap, gk.ap, AF.Silu), [gk], [st_])
                        kb.op("dve", lambda e, uk=uk, st_=st_, b=b, j=j: e.tensor_tensor(actT[2 * b + j].ap, uk.ap, st_.ap, ALU.mult),
                              [uk, st_], [actT[2 * b + j]])
                if gi + 1 < ngr:
                    for b in range(4):
                        kb.dma("pool", nTq[b].ap, ntscr[gi + 1, b], writes=[nTq[b]])
                for kg in range(3):
                    k0_ = 8 * kg
                    nk = min(8, FT - k0_)
                    for hf in range(2):
                        slot = wload(wS[20 + 2 * kg + hf])
                        s3 = slot.ap.rearrange("p (kt n) -> p kt n", n=512)
                        for c in range(4):
                            bk = psum[2 * c + hf]

                            def f2(e, bk=bk, s3=s3, c=c, k0_=k0_, nk=nk):
                                last = None
                                for kk in range(nk):
                                    kt = k0_ + kk
                                    last = e.matmul(bk.ap, actT[kt].ap[:, c * 128:(c + 1) * 128], s3[:, kk, :], start=(kt == 0), stop=(kt == FT - 1))
                                return last
                            kb.op("pe", f2, [slot] + actT[k0_:k0_ + nk], [bk])
                for c in range(4):
                    for hf in range(2):
                        bk = psum[2 * c + hf]
                        kb.op("dve", lambda e, bk=bk, hf=hf, c=c: e.tensor_tensor(xb[c].ap[:, hf * 512:(hf + 1) * 512], xb[c].ap[:, hf * 512:(hf + 1) * 512],
                                                                               bk.ap, ALU.add), [xb[c], bk], [xb[c]])
                    kb.op("act", lambda e, c=c: e.activation(xs[c].ap, xb[c].ap, AF.Square, accum_out=ssq[c].ap), [xb[c]], [xs[c], ssq[c]])
                    kb.op("act", lambda e, c=c: e.activation(rsd[c].ap, ssq[c].ap, AF.Sqrt, bias=EPS, scale=1.0 / D), [ssq[c]], [rsd[c]])
                    kb.op("dve", lambda e, c=c: e.reciprocal(rsd[c].ap, rsd[c].ap), [rsd[c]], [rsd[c]])
                    kb.op("dve", lambda e, c=c: e.scalar_tensor_tensor(xb[c].ap, xb[c].ap, rsd[c].ap[:, 0:1], gfin.ap, ALU.mult, ALU.mult),
                          [xb[c], rsd[c], gfin], [xb[c]])
                    kb.dma("act", yout[gi * 512 + c * 128: gi * 512 + (c + 1) * 128, :], xb[c].ap, reads=[xb[c]], owner=xst[gi % 2][c], is_out=True)
            kb.barrier()

    for (kind, si, S) in seqs:
        xin = I["xp" if kind == "p" else "xs"][si]
        yout = (X.yp if kind == "p" else X.ys)[si]
        nch = S // CH
        ngr = S // 512
        nsb = (S + NM) // NT
        nr = S // NT
        with ExitStack() as ph:
            ust = [T(sbt(ph, f"ust{i}", [128, 512], BF16)[:], key=f"pa_ust{i}") for i in range(2)]
            uc = 0
            um4 = umeta.ap.rearrange("p (kt s i) -> p kt i s", kt=4, i=NT)
            for kt in range(4):
                for i in range(NT):
                    kb.dma("pool", uscrA[kt * 8:(kt + 1) * 8, :, i, 0:2].rearrange("g c s -> (g c) s"), um4[:, kt, i, :], reads=[umeta],
                           allow_slow_non_contiguous=True)
            xg2_t = sbt(ph, "xg2", [128, 4 * D], F32)
            xbuf = [xg[0], [T(xg2_t[:, c * D:(c + 1) * D], key=f"pa_xg2_{c}") for c in range(4)]]

            def load_xa(gi_):
                xb_ = xbuf[gi_ % 2]
                for c in range(4):
                    kb.dma("pool", xb_[c].ap, xin[gi_ * 512 + c * 128: gi_ * 512 + (c + 1) * 128, :], writes=[xb_[c]])

            qA = [T(sbt(ph, f"qA{c}", [128, 512], BF16)[:]) for c in range(4)]
            sgA = [T(sbt(ph, f"sgA{c}", [128, H * DV], BF16)[:]) for c in range(4)]
            kwA = [kw, [T(sbt(ph, f"kw2_{c}", [128, 512], BF16)[:]) for c in range(4)]]
            vvA = [vv, [T(sbt(ph, f"vv2_{c}", [128, H * DV], BF16)[:]) for c in range(4)]]
            st_first = [True]

            def chain(gi):
                kw_, vv_ = kwA[gi % 2], vvA[gi % 2]
                for c in range(3, -1, -1):
                    n = gi * 4 + c
                    sb_ = stb_bf[n % 2]
                    if st_first[0]:
                        kb.op("dve", lambda e: e.memset(stb.ap, 0.0), [], [stb])
                        kb.op("pool", lambda e: e.memset(sb_.ap, 0.0), [], [sb_])
                    else:
                        kb.op("act", lambda e: e.activation(sb_.ap, stb.ap, AF.Copy), [stb], [sb_])
                    st_first[0] = False
                    kb.dma("pool", stash[n], sb_.ap, reads=[sb_])
                    if n == 0:
                        break
                    b0, b1 = ps_pair()

                    def kvb(e):
                        last = None
                        for h in range(H):
                            bk = b0 if h < 2 else b1
                            last = e.matmul(bk.ap[:, (h % 2) * 256:(h % 2 + 1) * 256], kw_[c].ap[:, h * 128:(h + 1) * 128],
                                            vv_[c].ap[:, h * 256:(h + 1) * 256], start=True, stop=True)
                        return last
                    kb.op("pe", kvb, [kw_[c], vv_[c]], [b0, b1])
                    for h in range(H):
                        bk = b0 if h < 2 else b1
                        kb.op("dve", lambda e: e.scalar_tensor_tensor(
                            stb.ap[:, h * 256:(h + 1) * 256], stb.ap[:, h * 256:(h + 1) * 256], wtab.ap[:, 20 + h:21 + h],
                            bk.ap[:, (h % 2) * 256:(h % 2 + 1) * 256], ALU.mult, ALU.add), [stb, wtab, bk], [stb])

            load_xa(ngr - 1)
            pending = None
            for gi in range(ngr - 1, -1, -1):
                xb = xbuf[gi % 2]
                kw_, vv_ = kwA[gi % 2], vvA[gi % 2]
                if gi - 1 >= 0:
                    load_xa(gi - 1)
                rmsnorm_to_nT(xb)
                for b in range(4):
                    kb.dma("act", ntscr[gi, b], nTq[b].ap, reads=[nTq[b]], owner=ntst[b])
                slot = wload(wS[0])
                for m in range(4):
                    bk = ps_next()
                    mm_feat(bk, slot, m * 128)
                    us = ust[uc % 2]
                    uc += 1
                    kb.op("act", lambda e: e.activation(us.ap.rearrange("p (i s) -> p i s", i=NT),
                                                        bk.ap.rearrange("p (s i) -> p i s", i=NT), AF.Copy), [bk], [us])
                    kb.dma("act", uscrA[m * 8:(m + 1) * 8, :, :, 2 + gi * 64: 2 + (gi + 1) * 64].rearrange("g c i s -> (g c) i s"),
                           us.ap.rearrange("p (i s) -> p i s", i=NT), reads=[us])
                slot = wload(wS[2])
                for c in range(4):
                    bk = ps_next()
                    mm_tok(bk, slot, c)
                    rope(bk, gi * 4 + c, kr[c])
                    kb.dma("pool", kscr[gi * 4 + c], kr[c].ap, reads=[kr[c]], owner=kst[c])
                    kb.op("pool", lambda e: e.tensor_tensor(kw_[c].ap.rearrange("p (h d) -> p h d", h=4),
                                                             kr[c].ap.rearrange("p (h d) -> p h d", h=4),
                                                             wtab.ap[:, 12:16].unsqueeze(2).broadcast_to([128, 4, 128]), ALU.mult),
                          [kr[c], wtab], [kw_[c]])
                for j in range(2):
                    slot = wload(wS[3 + j])
                    for c in range(4):
                        bk = ps_next()
                        mm_tok(bk, slot, c)
                        kb.op("act", lambda e: e.activation(vv_[c].ap[:, j * 512:(j + 1) * 512], bk.ap, AF.Copy), [bk], [vv_[c]])
                for c in range(4):
                    kb.dma("act", vscr[gi * 4 + c], vv_[c].ap, reads=[vv_[c]], owner=vst[(gi % 2) * 4 + c])
                slot = wload(wS[1])
                for c in range(4):
                    bk = ps_next()
                    mm_tok(bk, slot, c)
                    rope(bk, gi * 4 + c, qA[c])
                    kb.dma("pool", qscr[gi * 4 + c], qA[c].ap, reads=[qA[c]], owner=kst[c])
                for j in range(2):
                    slot = wload(wS[5 + j])
                    for c in range(4):
                        bk = ps_next()
                        mm_tok(bk, slot, c)
                        kb.op("act", lambda e: e.activation(sgA[c].ap[:, j * 512:(j + 1) * 512], bk.ap, AF.Silu), [bk], [sgA[c]])
                for c in range(4):
                    kb.dma("act", sgscr[gi * 4 + c], sgA[c].ap, reads=[sgA[c]], owner=vst[(gi % 2) * 4 + c])
                if pending is not None:
                    chain(pending)
                pending = gi
            chain(pending)
            kb.barrier()
        s5_stage(S, nsb, nr)
        pass_b(xin, yout, S, ngr)


_CACHE = {}


def _in_map(inp, consts, xp=None, xs=None):
    f32 = lambda a: np.ascontiguousarray(np.asarray(a, dtype=np.float32))
    m = {}
    if xp is not None:
        m["xp"] = f32(xp)
    if xs is not None:
        m["xs"] = f32(xs)
    m["meta"] = f32(inp["meta_tokens"])
    m["g_mix"] = f32(inp["norm_mix_g"]).reshape(D)
    m["w_in"] = f32(inp["w_in"]).reshape(D, 5632)
    m["lam_re"] = f32(inp["ssm_lam_re"]).reshape(2, NG, PS)
    m["lam_im"] = f32(inp["ssm_lam_im"]).reshape(2, NG, PS)
    m["log_step"] = f32(inp["ssm_log_step"]).reshape(2, NG)
    m["b_re"] = f32(inp["ssm_b_re"]).reshape(2, NG, PS, GC)
    m["b_im"] = f32(inp["ssm_b_im"]).reshape(2, NG, PS, GC)
    m["c_re"] = f32(inp["ssm_c_re"]).reshape(2, NG, GC, PS)
    m["c_im"] = f32(inp["ssm_c_im"]).reshape(2, NG, GC, PS)
    m["ssm_d"] = f32(inp["ssm_d"]).reshape(512)
    m["w_glu"] = f32(inp["w_ssm_glu"]).reshape(512, 2048)
    m["decay"] = f32(inp["ret_decay_logit"]).reshape(8)
    m["w_ro"] = f32(inp["w_ret_out"]).reshape(D, D)
    m["w_out"] = f32(inp["w_out"]).reshape(D, D)
    m["g_ffn"] = f32(inp["norm_ffn_g"]).reshape(D)
    m["w_f1"] = f32(inp["w_ffn_in"]).reshape(D, 2 * DFF)
    m["w_f2"] = f32(inp["w_ffn_out"]).reshape(DFF, D)
    m["g_fin"] = f32(inp["norm_final_g"]).reshape(D)
    for k, v in consts.items():
        m["c_" + k] = f32(v)
    return m


def kernel(**inputs):
    xp = np.asarray(inputs["x_prompt"], dtype=np.float32)
    xs = np.asarray(inputs["x_sample"], dtype=np.float32)
    n = 8
    npc = xp.shape[0] // n
    nsc = xs.shape[0] // n
    key = (npc, xp.shape[1], nsc, xs.shape[1])
    if key not in _CACHE:
        _CACHE[key] = build(npc, xp.shape[1], nsc, xs.shape[1])
    nc, consts, _ = _CACHE[key]
    in_maps = [_in_map(inputs, consts, xp=xp[c * npc:(c + 1) * npc], xs=xs[c * nsc:(c + 1) * nsc]) for c in range(n)]
    res = run_bass_kernel_spmd(nc, in_maps, core_ids=list(range(n)))
    yp = np.concatenate([np.asarray(r["yp"], dtype=np.float32) for r in res.results], axis=0)
    ys = np.concatenate([np.asarray(r["ys"], dtype=np.float32) for r in res.results], axis=0)
    return (yp, ys)
```

```python
import math
from contextlib import ExitStack

import numpy as np
import concourse.bass as bass
import concourse.mybir as mybir
from concourse.bass_utils import run_bass_kernel_spmd

F32 = mybir.dt.float32
BF16 = mybir.dt.bfloat16
I32 = mybir.dt.int32
AF = mybir.ActivationFunctionType
ALU = mybir.AluOpType
AX = mybir.AxisListType

D = 1024
KT = 8
NM = 16
CH = 128
H = 4
DK = 128
DV = 256
NG = 32
GC = 16
PS = 64
DFF = 2816
FT = 22
NT = 8
EPS = 1e-6
TWO_PI = 2.0 * math.pi
C1 = 6.28125
C2 = TWO_PI - 6.28125
OFF_U, OFF_Q, OFF_K, OFF_V, OFF_GR, OFF_GA, OFF_GB = 0, 512, 1024, 1536, 2560, 3584, 4608
N_WS = 26


class T:
    __slots__ = ("ap", "w", "r", "dsem", "dcnt", "closed", "key")

    def __init__(self, ap, key=None):
        self.ap = ap
        self.key = key
        self.w = None
        self.r = {}
        self.dsem = None
        self.dcnt = 0
        self.closed = None


class KB:
    def __init__(self, nc, es):
        self.nc = nc
        self.es = es
        self.E = {"pe": nc.tensor, "act": nc.scalar, "dve": nc.vector, "pool": nc.gpsimd, "sp": nc.sync}
        self.nsem = 0
        self.sem = {}
        self.cnt = {}
        for e in ("pe", "act", "dve", "pool"):
            self._new_epoch(e)
        self.seen = {e: {} for e in self.E}
        self.dma_evs = []
        self.out_evs = []
        self.nwaits = 0
        self.dsem_pool = {}

    def new_sem(self, name):
        self.nsem += 1
        return self.es.enter_context(self.nc.semaphore(f"{name}{self.nsem}"))

    def _new_epoch(self, e):
        self.sem[e] = self.new_sem("c" + e)
        self.cnt[e] = 0

    def _wait(self, eng, ev):
        sem, val, _, owner = ev
        if owner is not None:
            val = owner.dcnt
            owner.closed = val
        s = self.seen[eng]
        if s.get(id(sem), 0) >= val:
            return
        s[id(sem)] = val
        self.nwaits += 1
        self.E[eng].wait_ge(sem, val)

    def op(self, eng, fn, reads=(), writes=()):
        evs = []
        for t in reads:
            if t.w is not None:
                evs.append(t.w)
        for t in writes:
            if t.w is not None:
                evs.append(t.w)
            evs.extend(t.r.values())
        for ev in evs:
            self._wait(eng, ev)
        inst = fn(self.E[eng])
        if self.cnt[eng] >= 30000:
            self._new_epoch(eng)
        self.cnt[eng] += 1
        inst.then_inc(self.sem[eng], 1)
        ev = (self.sem[eng], self.cnt[eng], eng, None)
        for t in reads:
            t.r[eng] = ev
        for t in writes:
            t.w = ev
            t.r = {}
        return ev

    def dma(self, q, out_ap, in_ap, reads=(), writes=(), owner=None, is_out=False, group=False, **kw):
        owner = owner or (writes[0] if writes else reads[0])
        if owner.dsem is None:
            if owner.key is not None and owner.key in self.dsem_pool:
                owner.dsem, owner.dcnt = self.dsem_pool[owner.key]
            else:
                owner.dsem = self.new_sem("d")
                owner.dcnt = 0
        src = ("dma", id(owner.dsem))
        evs = []
        for t in reads:
            if t.w is not None:
                evs.append(t.w)
        for t in writes:
            if t.w is not None and not (group and t.w[2] == src):
                evs.append(t.w)
            for s2, ev in t.r.items():
                if not (group and s2 == src):
                    evs.append(ev)
        for ev in evs:
            self._wait(q, ev)
        if owner.closed is not None:
            self._wait(q, (owner.dsem, owner.closed, src, owner))
            owner.closed = None
        inst = self.E[q].dma_start(out=out_ap, in_=in_ap, **kw)
        owner.dcnt += 16
        if owner.key is not None:
            self.dsem_pool[owner.key] = (owner.dsem, owner.dcnt)
        inst.then_inc(owner.dsem, 16)
        ev = (owner.dsem, owner.dcnt, src, owner)
        for t in reads:
            t.r[src] = ev
        for t in writes:
            t.w = ev
            t.r = {}
        self.dma_evs.append(ev)
        if is_out:
            self.out_evs.append(ev)
        return ev

    def barrier(self):
        evs = [(self.sem[e], self.cnt[e], e, None) for e in ("pe", "act", "dve", "pool") if self.cnt[e] > 0]
        last = {}
        for ev in self.dma_evs:
            last[ev[2]] = ev
        evs += list(last.values())
        for eng in ("pe", "act", "dve", "pool", "sp"):
            for ev in evs:
                if ev[2] != eng:
                    self._wait(eng, ev)
        self.dma_evs = []

    def finish(self):
        last = {}
        for ev in self.out_evs:
            last[ev[2]] = ev
        for ev in last.values():
            self._wait("sp", ev)


def make_consts(nch_max, nsb_max):
    c = {}
    c["ident"] = np.eye(128, dtype=np.float32)
    pr = np.arange(128)
    pos = np.zeros((128, nch_max + 1), np.float32)
    for n in range(nch_max):
        pos[:, n] = NM + CH * n + pr
    pos[:, nch_max] = pr
    c["pos"] = pos
    inv = (np.float32(10000.0) ** (-np.arange(64, dtype=np.float32) / np.float32(64))).astype(np.float32)
    c["fidx"] = np.tile(inv[None, :], (128, 1))
    c["sbidx"] = np.tile(np.arange(nsb_max, dtype=np.float32)[None, :], (128, 1))
    ii = pr // GC
    c["maskf"] = (ii[None, :] >= ii[:, None]).astype(np.float32)
    c["maskb"] = (ii[:, None] >= ii[None, :]).astype(np.float32)
    sp_, cc = pr[:, None], pr[None, :]
    c["dmf"] = (cc >= sp_).astype(np.float32)
    c["dmb"] = (cc < sp_).astype(np.float32)
    c["ef"] = np.maximum(cc - sp_, 0).astype(np.float32)
    c["eb"] = np.maximum(sp_ - cc, 0).astype(np.float32)
    misc = np.zeros((128, 8), np.float32)
    misc[:, 0] = pr + 1.0
    misc[:, 1] = CH - 1.0 - pr
    misc[:, 2] = CH - pr
    misc[:, 3] = pr
    misc[:, 4] = NM - 1.0 - pr
    misc[:, 5] = np.where(pr < 64, 1.0, -1.0)
    misc[:, 6] = -misc[:, 5]
    misc[:, 7] = float(CH)
    c["misc"] = misc
    kexp = np.tile(np.arange(-7, 9, dtype=np.float32)[None, :], (128, 1))
    c["kexp"] = kexp
    return c


CONST_SHAPES = None


def build(NP_, SP_, NS_, SS_, dbg=None):
    dbg = dbg or set()
    nc = bass.Bass("TRN2", target_bir_lowering=False)
    nch_max = max(SP_ if NP_ else 0, SS_ if NS_ else 0) // CH
    nsb_max = (max(SP_ if NP_ else 0, SS_ if NS_ else 0) + NM) // NT
    consts = make_consts(nch_max, nsb_max)
    es = ExitStack()
    with es:
        kb = KB(nc, es)

        def din(name, shape, dt=F32):
            return nc.dram_tensor(name, list(shape), dt, kind="ExternalInput").ap()

        def dout(name, shape, dt=F32):
            return nc.dram_tensor(name, list(shape), dt, kind="ExternalOutput").ap()

        def dscr(name, shape, dt):
            if "scratch" in dbg:
                return nc.dram_tensor(name, list(shape), dt, kind="ExternalOutput").ap()
            return nc.dram_tensor(name, list(shape), dt).ap()

        uniq = [0]

        def sbt(stack, name, shape, dt):
            uniq[0] += 1
            return stack.enter_context(nc.sbuf_tensor(f"{name}_{uniq[0]}", list(shape), dt))

        I = {}
        if NP_:
            I["xp"] = din("xp", [NP_, SP_, D])
            yp = dout("yp", [NP_, SP_, D])
        if NS_:
            I["xs"] = din("xs", [NS_, SS_, D])
            ys = dout("ys", [NS_, SS_, D])
        I["meta"] = din("meta", [NM, D])
        I["g_mix"] = din("g_mix", [D])
        I["w_in"] = din("w_in", [D, 5632])
        I["lam_re"] = din("lam_re", [2, NG, PS])
        I["lam_im"] = din("lam_im", [2, NG, PS])
        I["log_step"] = din("log_step", [2, NG])
        I["b_re"] = din("b_re", [2, NG, PS, GC])
        I["b_im"] = din("b_im", [2, NG, PS, GC])
        I["c_re"] = din("c_re", [2, NG, GC, PS])
        I["c_im"] = din("c_im", [2, NG, GC, PS])
        I["ssm_d"] = din("ssm_d", [512])
        I["w_glu"] = din("w_glu", [512, 2048])
        I["decay"] = din("decay", [8])
        I["w_ro"] = din("w_ro", [D, D])
        I["w_out"] = din("w_out", [D, D])
        I["g_ffn"] = din("g_ffn", [D])
        I["w_f1"] = din("w_f1", [D, 2 * DFF])
        I["w_f2"] = din("w_f2", [DFF, D])
        I["g_fin"] = din("g_fin", [D])
        CI = {k: din("c_" + k, v.shape) for k, v in consts.items()}

        wS = dscr("wS", [N_WS, 128, 8 * 512], BF16)
        wM = dscr("wM", [8, 128, 4096], BF16)
        s5w = dscr("s5w", [64, 128, 512], BF16)
        s5m = dscr("s5m", [NG, 128, 128], BF16)
        rot = dscr("rot", [64, 128, 2, nsb_max], F32)
        DBG = {}

        def dbg_out(name, src_ap, shape, reads, dt=F32):
            if name not in dbg:
                return
            o = dout("dbg_" + name, shape, dt)
            DBG[name] = o
            kb.dma("sp", o, src_ap, reads=reads, is_out=True)

        top = ExitStack()
        es.enter_context(top)
        psum = [T(es.enter_context(nc.psum_tensor(f"ps{b}", [128, 512], F32))[:]) for b in range(8)]
        ps_rr = [0]

        def ps_next():
            b = ps_rr[0] % 8
            ps_rr[0] += 1
            return psum[b]

        def ps_pair():
            if ps_rr[0] % 2:
                ps_rr[0] += 1
            b = ps_rr[0] % 8
            ps_rr[0] += 2
            return psum[b], psum[b + 1]

        def ctile(name, shape, dt=F32):
            return T(sbt(top, name, shape, dt)[:])

        ident_f = ctile("ident_f", [128, 128])
        ident_b = ctile("ident_b", [128, 128], BF16)
        misc = ctile("misc", [128, 8])
        ropec = ctile("ropec", [128, (nch_max + 1) * 64])
        ropes = ctile("ropes", [128, (nch_max + 1) * 64])
        dmT = ctile("dmT", [128, 512])
        wtab = ctile("wtab", [128, 32])
        gfin = ctile("gfin", [128, D])
        st0 = ctile("st0", [128, H * DV])
        umeta = ctile("umeta", [128, 4 * NM], BF16)
        cload = T(None)

        def cdma(dst_t, dst_ap, src_ap, **kw):
            kb.dma("sp", dst_ap, src_ap, writes=[dst_t], owner=cload, group=True, **kw)

        cdma(ident_f, ident_f.ap, CI["ident"])
        cdma(misc, misc.ap, CI["misc"])
        cdma(gfin, gfin.ap, I["g_fin"].partition_broadcast(128))
        kb.op("dve", lambda e: e.tensor_copy(ident_b.ap, ident_f.ap), [ident_f], [ident_b])

        def sincos(eng_v, ang, n, sin_out, cos_out, tmp_f, tmp_i, sin_scale=None):
            a = ang.ap
            tf, ti = tmp_f.ap, tmp_i.ap
            kb.op(eng_v, lambda e: e.tensor_scalar(tf, a, 1.0 / TWO_PI, None, ALU.mult), [ang], [tmp_f])
            kb.op(eng_v, lambda e: e.tensor_copy(ti, tf), [tmp_f], [tmp_i])
            kb.op(eng_v, lambda e: e.tensor_copy(tf, ti), [tmp_i], [tmp_f])
            kb.op(eng_v, lambda e: e.scalar_tensor_tensor(a, tf, -C1, a, ALU.mult, ALU.add), [tmp_f, ang], [ang])
            kb.op(eng_v, lambda e: e.scalar_tensor_tensor(a, tf, -C2, a, ALU.mult, ALU.add), [tmp_f, ang], [ang])
            kb.op(eng_v, lambda e: e.tensor_scalar(a, a, math.pi, -math.pi, ALU.min, ALU.max), [ang], [ang])
            if sin_out is not None:
                if sin_scale is None:
                    kb.op("act", lambda e: e.activation(sin_out.ap, a, AF.Sin), [ang], [sin_out])
                else:
                    kb.op("act", lambda e: e.activation(tf, a, AF.Sin), [ang], [tmp_f])
                    kb.op("act", lambda e: e.activation(sin_out.ap, tf, AF.Copy, scale=sin_scale[1]),
                          [tmp_f, sin_scale[0]], [sin_out])
            kb.op(eng_v, lambda e: e.tensor_scalar(tf, a, math.pi / 2, TWO_PI, ALU.is_gt, ALU.mult), [ang], [tmp_f])
            kb.op(eng_v, lambda e: e.scalar_tensor_tensor(a, a, math.pi / 2, tf, ALU.add, ALU.subtract), [ang, tmp_f], [ang])
            kb.op(eng_v, lambda e: e.tensor_scalar(a, a, math.pi, -math.pi, ALU.min, ALU.max), [ang], [ang])
            kb.op("act", lambda e: e.activation(cos_out.ap, a, AF.Sin), [ang], [cos_out])

        with ExitStack() as ph:
            def ptile(name, shape, dt=F32):
                return T(sbt(ph, name, shape, dt)[:])
            ncol = (nch_max + 1) * 64
            pos = ptile("pos", [128, nch_max + 1])
            fidx = ptile("fidx", [128, 64])
            inv = ptile("inv", [128, 64])
            ang = ptile("ang", [128, ncol])
            tf = ptile("tf", [128, ncol])
            ti = ptile("ti", [128, ncol], I32)
            cdma(pos, pos.ap, CI["pos"])
            cdma(fidx, fidx.ap, CI["fidx"])
            kb.op("dve", lambda e: e.tensor_copy(inv.ap, fidx.ap), [fidx], [inv])
            a3 = ang.ap.rearrange("p (n f) -> p n f", f=64)
            kb.op("dve", lambda e: e.tensor_tensor(
                a3, inv.ap.unsqueeze(1).broadcast_to([128, nch_max + 1, 64]),
                pos.ap.unsqueeze(2).broadcast_to([128, nch_max + 1, 64]), ALU.mult), [inv, pos], [ang])
            sincos("dve", ang, ncol, ropes, ropec, tf, ti)
            dl = ptile("dl", [128, 8])
            lg = ptile("lg", [128, 8])
            cdma(dl, dl.ap, I["decay"].partition_broadcast(128))
            kb.op("act", lambda e: e.activation(lg.ap, dl.ap, AF.Exp, scale=-1.0), [dl], [lg])
            kb.op("act", lambda e: e.activation(lg.ap, lg.ap, AF.Ln, bias=1.0), [lg], [lg])
            kb.op("dve", lambda e: e.tensor_scalar(lg.ap, lg.ap, -1.0, None, ALU.mult), [lg], [lg])
            wt = wtab.ap
            m = misc.ap
            specs = [(0, 0, 0), (1, 0, 1), (2, 4, 2), (3, 4, 3), (4, 0, 7), (5, 4, 7), (6, 0, 4)]
            for (slot, lgo, mcol) in specs:
                kb.op("dve", lambda e, slot=slot, lgo=lgo, mcol=mcol: e.tensor_scalar(
                    wt[:, slot * 4:(slot + 1) * 4], lg.ap[:, lgo:lgo + 4], m[:, mcol:mcol + 1], None, ALU.mult),
                    [lg, misc], [wtab])
            kb.op("act", lambda e: e.activation(wt[:, 0:28], wt[:, 0:28], AF.Exp), [wtab], [wtab])
            ef = ptile("ef", [128, 128]); eb = ptile("eb", [128, 128])
            dmf = ptile("dmf", [128, 128]); dmb = ptile("dmb", [128, 128])
            t1 = ptile("t1", [128, 128]); t2 = ptile("t2", [128, 128])
            cdma(ef, ef.ap, CI["ef"]); cdma(eb, eb.ap, CI["eb"])
            cdma(dmf, dmf.ap, CI["dmf"]); cdma(dmb, dmb.ap, CI["dmb"])
            for h in range(H):
                kb.op("act", lambda e, h=h: e.activation(t1.ap, ef.ap, AF.Exp, scale=lg.ap[:, h:h + 1]), [ef, lg], [t1])
                kb.op("act", lambda e, h=h: e.activation(t2.ap, eb.ap, AF.Exp, scale=lg.ap[:, 4 + h:5 + h]), [eb, lg], [t2])
                kb.op("dve", lambda e: e.tensor_tensor(t1.ap, t1.ap, dmf.ap, ALU.mult), [t1, dmf], [t1])
                kb.op("dve", lambda e: e.tensor_tensor(t2.ap, t2.ap, dmb.ap, ALU.mult), [t2, dmb], [t2])
                kb.op("dve", lambda e, h=h: e.tensor_tensor(dmT.ap[:, h * 128:(h + 1) * 128], t1.ap, t2.ap, ALU.add),
                      [t1, t2], [dmT])
            dbg_out("ropec", ropec.ap, [128, ncol], [ropec])
            dbg_out("ropes", ropes.ap, [128, ncol], [ropes])
            dbg_out("wtab", wtab.ap[:, 0:28], [128, 28], [wtab])
            dbg_out("dmT", dmT.ap, [128, 512], [dmT])
            kb.barrier()

        with ExitStack() as ph:
            def ptile(name, shape, dt=F32):
                return T(sbt(ph, name, shape, dt)[:])
            gm = ptile("gm", [128, KT]); gf = ptile("gf", [128, KT]); gq = ptile("gq", [128, KT])
            cdma(gm, gm.ap, I["g_mix"].rearrange("(kt p) -> p kt", p=128), allow_slow_non_contiguous=True)
            cdma(gf, gf.ap, I["g_ffn"].rearrange("(kt p) -> p kt", p=128), allow_slow_non_contiguous=True)
            kb.op("dve", lambda e: e.tensor_scalar(gq.ap, gm.ap, DK ** -0.5, None, ALU.mult), [gm], [gq])
            NST = 4
            st32 = [ptile(f"st32_{i}", [128, 4096]) for i in range(NST)]
            st16 = [ptile(f"st16_{i}", [128, 4096], BF16) for i in range(NST)]
            cnt = [0]

            def cast_rows(i, kts, col0, ncols, width, scale_t):
                s3 = st32[i].ap.rearrange("p (kt c) -> p kt c", c=width)
                d3 = st16[i].ap.rearrange("p (kt c) -> p kt c", c=width)
                for kt in kts:
                    eng = ("act", "dve")[cnt[0] % 2]
                    cnt[0] += 1
                    src_ = s3[:, kt, col0:col0 + ncols]
                    dst_ = d3[:, kt, col0:col0 + ncols]
                    if scale_t is None:
                        if eng == "act":
                            kb.op("act", lambda e: e.activation(dst_, src_, AF.Copy), [st32[i]], [st16[i]])
                        else:
                            kb.op(eng, lambda e: e.tensor_copy(dst_, src_), [st32[i]], [st16[i]])
                    else:
                        if eng == "act":
                            kb.op("act", lambda e: e.activation(dst_, src_, AF.Copy, scale=scale_t.ap[:, kt:kt + 1]), [st32[i], scale_t], [st16[i]])
                        else:
                            kb.op(eng, lambda e: e.tensor_scalar(dst_, src_, scale_t.ap[:, kt:kt + 1], None, ALU.mult), [st32[i], scale_t], [st16[i]])

            blk = [0]

            def load(i, src, nkt, col0, ncols, width, dcol0):
                s3 = st32[i].ap.rearrange("p (kt c) -> p kt c", c=width)
                kb.dma("sp", s3[:, 0:nkt, dcol0:dcol0 + ncols],
                       src.rearrange("(kt p) c -> p kt c", p=128)[:, 0:nkt, col0:col0 + ncols], writes=[st32[i]], group=True)

            jobs = []

            def conv_block(dst_ap, parts, width=512, zero_tail=False):
                i = blk[0] % NST
                blk[0] += 1

                def do_load():
                    for (src, nkt, c0, ncols, d0, sc, r0) in parts:
                        if r0:
                            s3 = st32[i].ap.rearrange("p (kt c) -> p kt c", c=width)
                            kb.dma("sp", s3[:, 0:nkt, d0:d0 + ncols],
                                   src.rearrange("(kt p) c -> p kt c", p=128)[:, r0:r0 + nkt, c0:c0 + ncols], writes=[st32[i]], group=True)
                        else:
                            load(i, src, nkt, c0, ncols, width, d0)

                def do_rest():
                    for (src, nkt, c0, ncols, d0, sc, r0) in parts:
                        cast_rows(i, range(nkt), d0, ncols, width, sc)
                    if zero_tail:
                        nk = parts[0][1]
                        kb.op("dve", lambda e: e.memset(st16[i].ap[:, nk * width:], 0.0), [], [st16[i]])
                    kb.dma("sp", dst_ap, st16[i].ap, reads=[st16[i]])
                jobs.append((do_load, do_rest))

            WI = I["w_in"]
            conv_block(wS[0], [(WI, 8, OFF_U, 512, 0, gm, 0)])
            conv_block(wS[1], [(WI, 8, OFF_Q, 512, 0, gq, 0)])
            conv_block(wS[2], [(WI, 8, OFF_K, 512, 0, gm, 0)])
            for j in range(2):
                conv_block(wS[3 + j], [(WI, 8, OFF_V + 512 * j, 512, 0, gm, 0)])
                conv_block(wS[5 + j], [(WI, 8, OFF_GR + 512 * j, 512, 0, gm, 0)])
                conv_block(wS[7 + j], [(I["w_out"], 8, 512 * j, 512, 0, None, 0)])
            for b in range(11):
                conv_block(wS[9 + b], [(I["w_f1"], 8, 256 * b, 256, 0, gf, 0),
                                       (I["w_f1"], 8, DFF + 256 * b, 256, 256, gf, 0)])
            for kg in range(3):
                nk = min(8, FT - 8 * kg)
                for hf in range(2):
                    conv_block(wS[20 + 2 * kg + hf], [(I["w_f2"], nk, 512 * hf, 512, 0, None, 8 * kg)], zero_tail=(nk < 8))
            def merge_job(mt):
                i = blk[0] % NST
                blk[0] += 1
                a3 = st32[i].ap[:, 0:3072].rearrange("p (kt c) -> p kt c", c=384)
                g3 = st32[i].ap[:, 3072:4096].rearrange("p (kt c) -> p kt c", c=256)
                o3 = st16[i].ap[:, 0:3072].rearrange("p (kt c) -> p kt c", c=384)
                og3 = st16[i].ap[:, 3072:4096].rearrange("p (kt c) -> p kt c", c=256)

                def src3(src, c0):
                    return src.rearrange("(kt p) c -> p kt c", p=128)[:, :, c0:c0 + 128]

                def do_load():
                    kb.dma("sp", a3[:, :, 0:128], src3(I["w_ro"], mt * 128), writes=[st32[i]], group=True)
                    kb.dma("sp", a3[:, :, 128:256], src3(WI, OFF_GA + mt * 128), writes=[st32[i]], group=True)
                    kb.dma("sp", a3[:, :, 256:384], src3(WI, OFF_GB + mt * 128), writes=[st32[i]], group=True)
                    kb.dma("sp", g3[:, :, 0:128], src3(I["w_glu"], mt * 128), writes=[st32[i]], group=True)
                    kb.dma("sp", g3[:, :, 128:256], src3(I["w_glu"], 1024 + mt * 128), writes=[st32[i]], group=True)

                def do_rest():
                    for kt in range(8):
                        kb.op("act", lambda e: e.activation(o3[:, kt, 0:128], a3[:, kt, 0:128], AF.Copy), [st32[i]], [st16[i]])
                        kb.op("dve", lambda e: e.tensor_scalar(o3[:, kt, 128:384], a3[:, kt, 128:384], gm.ap[:, kt:kt + 1], None, ALU.mult),
                              [st32[i], gm], [st16[i]])
                    kb.op("act", lambda e: e.activation(og3, g3, AF.Copy), [st32[i]], [st16[i]])
                    kb.dma("sp", wM[mt], st16[i].ap, reads=[st16[i]])
                jobs.append((do_load, do_rest))

            for mt in range(8):
                merge_job(mt)
            LOOK = NST - 1
            for j in range(min(LOOK, len(jobs))):
                jobs[j][0]()
            for j in range(len(jobs)):
                if j + LOOK < len(jobs):
                    jobs[j + LOOK][0]()
                jobs[j][1]()
            kb.barrier()

        import types
        X = types.SimpleNamespace(**{k: v for k, v in locals().items() if k != "X"})
        build_s5_params(X)
        build_main(X)
        kb.finish()
    return nc, consts, DBG


def build_s5_params(X):
    nc, kb, I, CI, sbt, cdma, sincos = X.nc, X.kb, X.I, X.CI, X.sbt, X.cdma, X.sincos
    psum, ps_next, ident_f, misc = X.psum, X.ps_next, X.ident_f, X.misc
    nsb = X.nsb_max
    top = X.top
    X.rho = T(sbt(top, "rho", [128, 64], F32)[:])
    rho = X.rho
    with ExitStack() as ph:
        def ptile(name, shape, dt=F32):
            return T(sbt(ph, name, shape, dt)[:])
        kexp = ptile("kexp", [128, 16])
        sbidx = ptile("sbidx", [128, nsb])
        maskf = ptile("maskf", [128, 128]); maskb = ptile("maskb", [128, 128])
        cdma(kexp, kexp.ap, CI["kexp"]); cdma(sbidx, sbidx.ap, CI["sbidx"])
        cdma(maskf, maskf.ap, CI["maskf"]); cdma(maskb, maskb.ap, CI["maskb"])
        dtab = ptile("dtab", [128, NG])
        for i in range(NT):
            cdma(dtab, dtab.ap[i * GC:(i + 1) * GC, :], I["ssm_d"].rearrange("(g c) -> c g", c=GC),
                 allow_slow_non_contiguous=True)
        macc = ptile("macc", [128, NG * 128])
        phi = ptile("phi", [128, 64])
        lr = ptile("lr", [128, NG]); li = ptile("li", [128, NG]); dtb = ptile("dtb", [128, NG])
        xx = ptile("xx", [128, NG]); th = ptile("th", [128, NG])
        bre = ptile("bre", [128, NG * GC]); bim = ptile("bim", [128, NG * GC])
        cre = ptile("cre", [128, NG * GC]); cim = ptile("cim", [128, NG * GC])
        cin = ptile("cin", [128, 128])
        xk = ptile("xk", [128, NG * 16]); thk = ptile("thk", [128, NG * 16])
        tfk = ptile("tfk", [128, NG * 16]); tik = ptile("tik", [128, NG * 16], I32)
        AR = ptile("AR", [128, NG * 16]); AI = ptile("AI", [128, NG * 16])
        tabs = {n: ptile("tab_" + n, [128, NG * 16]) for n in
                ("s1nat", "s2nat", "s1swp", "s2swp", "s1far", "s2far", "s1fsw", "s2fsw")}
        t_a = ptile("t_a", [128, NG]); t_b = ptile("t_b", [128, NG]); t_c = ptile("t_c", [128, NG])
        coefr = ptile("coefr", [128, NG]); coefi = ptile("coefi", [128, NG])
        bbr = ptile("bbr", [128, NG * GC]); bbi = ptile("bbi", [128, NG * GC])
        tb1 = ptile("tb1", [128, NG * GC])
        Z = {n: ptile("Z_" + n, [128, NG * 128]) for n in ("wnat", "wswp", "v", "far", "fsw")}
        ztmp = ptile("ztmp", [128, NG * 128])
        stage = [ptile(f"s5stage{i}", [128, 512], BF16) for i in range(2)]
        rstage = [ptile(f"rstage{i}", [128, 2 * nsb]) for i in range(2)]
        rang = ptile("rang", [128, nsb]); rtf = ptile("rtf", [128, nsb]); rti = ptile("rti", [128, nsb], I32)
        mstage = T(Z["fsw"].ap.bitcast(BF16)[:, 0:NG * 128])

        def v3(t, w):
            return t.ap.rearrange("p (g k) -> p g k", k=w)

        for d in range(2):
            for half in range(2):
                sl = slice(half * 64, half * 64 + 64)
                cdma(lr, lr.ap[sl, :], I["lam_re"][d].rearrange("g p -> p g"), allow_slow_non_contiguous=True)
                cdma(li, li.ap[sl, :], I["lam_im"][d].rearrange("g p -> p g"), allow_slow_non_contiguous=True)
                cdma(bre, bre.ap[sl, :].rearrange("p (g c) -> p g c", c=GC), I["b_re"][d].rearrange("g p c -> p g c"))
                cdma(bim, bim.ap[sl, :].rearrange("p (g c) -> p g c", c=GC), I["b_im"][d].rearrange("g p c -> p g c"))
            cdma(dtb, dtb.ap, I["log_step"][d].partition_broadcast(128))
            for (src, dst) in ((I["c_re"], cre), (I["c_im"], cim)):
                flat = src[d].rearrange("g c p -> (g c) p")
                for j in range(4):
                    cdma(cin, cin.ap[:, 0:64], flat[j * 128:(j + 1) * 128, :])
                    cdma(cin, cin.ap[:, 64:128], flat[j * 128:(j + 1) * 128, :])
                    pt = ps_next()
                    kb.op("pe", lambda e: e.transpose(pt.ap[:, 0:128], cin.ap, ident_f.ap), [cin, ident_f], [pt])
                    kb.op("act", lambda e, j=j: e.activation(dst.ap[:, j * 128:(j + 1) * 128], pt.ap[:, 0:128], AF.Copy), [pt], [dst])
            kb.op("act", lambda e: e.activation(dtb.ap, dtb.ap, AF.Exp), [dtb], [dtb])
            kb.op("dve", lambda e: e.tensor_tensor(xx.ap, lr.ap, dtb.ap, ALU.mult), [lr, dtb], [xx])
            kb.op("dve", lambda e: e.tensor_tensor(th.ap, li.ap, dtb.ap, ALU.mult), [li, dtb], [th])
            kb.op("dve", lambda e: e.tensor_scalar(phi.ap[:, d * NG:(d + 1) * NG], th.ap, float(NT), None, ALU.mult), [th], [phi])
            kx = kexp.ap.unsqueeze(1).broadcast_to([128, NG, 16])
            kb.op("dve", lambda e: e.tensor_tensor(v3(xk, 16), xx.ap.unsqueeze(2).broadcast_to([128, NG, 16]), kx, ALU.mult),
                  [xx, kexp], [xk])
            kb.op("dve", lambda e: e.tensor_tensor(v3(thk, 16), th.ap.unsqueeze(2).broadcast_to([128, NG, 16]), kx, ALU.mult),
                  [th, kexp], [thk])
            kb.op("act", lambda e: e.activation(xk.ap, xk.ap, AF.Exp), [xk], [xk])
            kb.op("act", lambda e: e.activation(rho.ap[:, d * NG:(d + 1) * NG], v3(xk, 16)[:, :, 15], AF.Copy), [xk], [rho])
            sincos("dve", thk, NG * 16, AI, AR, tfk, tik)
            kb.op("dve", lambda e: e.tensor_tensor(AR.ap, AR.ap, xk.ap, ALU.mult), [AR, xk], [AR])
            kb.op("dve", lambda e: e.tensor_tensor(AI.ap, AI.ap, xk.ap, ALU.mult), [AI, xk], [AI])
            ar1 = v3(AR, 16)[:, :, 8]
            ai1 = v3(AI, 16)[:, :, 8]
            kb.op("dve", lambda e: e.tensor_scalar(t_a.ap, ar1, -1.0, None, ALU.add), [AR], [t_a])
            kb.op("dve", lambda e: e.tensor_tensor(t_b.ap, lr.ap, lr.ap, ALU.mult), [lr], [t_b])
            kb.op("dve", lambda e: e.tensor_tensor(t_c.ap, li.ap, li.ap, ALU.mult), [li], [t_c])
            kb.op("dve", lambda e: e.tensor_tensor(t_b.ap, t_b.ap, t_c.ap, ALU.add), [t_b, t_c], [t_b])
            kb.op("dve", lambda e: e.reciprocal(t_b.ap, t_b.ap), [t_b], [t_b])
            kb.op("dve", lambda e: e.tensor_tensor(coefr.ap, t_a.ap, lr.ap, ALU.mult), [t_a, lr], [coefr])
            kb.op("dve", lambda e: e.tensor_tensor(t_c.ap, ai1, li.ap, ALU.mult), [AI, li], [t_c])
            kb.op("dve", lambda e: e.tensor_tensor(coefr.ap, coefr.ap, t_c.ap, ALU.add), [coefr, t_c], [coefr])
            kb.op("dve", lambda e: e.tensor_tensor(coefr.ap, coefr.ap, t_b.ap, ALU.mult), [coefr, t_b], [coefr])
            kb.op("dve", lambda e: e.tensor_tensor(coefi.ap, ai1, lr.ap, ALU.mult), [AI, lr], [coefi])
            kb.op("dve", lambda e: e.tensor_tensor(t_c.ap, t_a.ap, li.ap, ALU.mult), [t_a, li], [t_c])
            kb.op("dve", lambda e: e.tensor_tensor(coefi.ap, coefi.ap, t_c.ap, ALU.subtract), [coefi, t_c], [coefi])
            kb.op("dve", lambda e: e.tensor_tensor(coefi.ap, coefi.ap, t_b.ap, ALU.mult), [coefi, t_b], [coefi])
            cr_b = coefr.ap.unsqueeze(2).broadcast_to([128, NG, GC])
            ci_b = coefi.ap.unsqueeze(2).broadcast_to([128, NG, GC])
            kb.op("dve", lambda e: e.tensor_tensor(v3(bbr, GC), v3(bre, GC), cr_b, ALU.mult), [bre, coefr], [bbr])
            kb.op("dve", lambda e: e.tensor_tensor(v3(tb1, GC), v3(bim, GC), ci_b, ALU.mult), [bim, coefi], [tb1])
            kb.op("dve", lambda e: e.tensor_tensor(bbr.ap, bbr.ap, tb1.ap, ALU.subtract), [bbr, tb1], [bbr])
            kb.op("dve", lambda e: e.tensor_tensor(v3(bbi, GC), v3(bim, GC), cr_b, ALU.mult), [bim, coefr], [bbi])
            kb.op("dve", lambda e: e.tensor_tensor(v3(tb1, GC), v3(bre, GC), ci_b, ALU.mult), [bre, coefi], [tb1])
            kb.op("dve", lambda e: e.tensor_tensor(bbi.ap, bbi.ap, tb1.ap, ALU.add), [bbi, tb1], [bbi])
            roles = {"s1nat": ((AR, 1), (AI, 1)), "s2nat": ((AI, -1), (AR, 1)),
                     "s1swp": ((AI, 1), (AR, 1)), "s2swp": ((AR, 1), (AI, -1)),
                     "s1far": ((AR, 1), (AI, -1)), "s2far": ((AI, -1), (AR, -1)),
                     "s1fsw": ((AI, -1), (AR, 1)), "s2fsw": ((AR, -1), (AI, -1))}
            for n, (lo, hi) in roles.items():
                for (src, sg), sl in ((lo, slice(0, 64)), (hi, slice(64, 128))):
                    kb.op("act", lambda e, src=src, sg=sg, sl=sl, n=n: e.activation(
                        tabs[n].ap[sl, :], src.ap[sl, :], AF.Copy, scale=float(sg)), [src], [tabs[n]])
            if d == 0:
                k_w, k_v, k_f = slice(14, 6, -1), slice(0, 8), slice(8, 16)
            else:
                k_w, k_v, k_f = slice(7, 15), slice(7, None, -1), slice(15, 7, -1)

            def zbuild(zn, xr, xi, s1, s2, ksl):
                z4 = Z[zn].ap.rearrange("p (g i c) -> p g i c", i=NT, c=GC)
                t4 = ztmp.ap.rearrange("p (g i c) -> p g i c", i=NT, c=GC)
                xr4 = v3(xr, GC).unsqueeze(2).broadcast_to([128, NG, NT, GC])
                xi4 = v3(xi, GC).unsqueeze(2).broadcast_to([128, NG, NT, GC])
                s14 = v3(tabs[s1], 16)[:, :, ksl].unsqueeze(3).broadcast_to([128, NG, NT, GC])
                s24 = v3(tabs[s2], 16)[:, :, ksl].unsqueeze(3).broadcast_to([128, NG, NT, GC])
                kb.op("dve", lambda e: e.tensor_tensor(z4, xr4, s14, ALU.mult), [xr, tabs[s1]], [Z[zn]])
                kb.op("dve", lambda e: e.tensor_tensor(t4, xi4, s24, ALU.mult), [xi, tabs[s2]], [ztmp])
                kb.op("dve", lambda e: e.tensor_tensor(Z[zn].ap, Z[zn].ap, ztmp.ap, ALU.add), [Z[zn], ztmp], [Z[zn]])

            zbuild("wnat", bbr, bbi, "s1nat", "s2nat", k_w)
            zbuild("wswp", bbr, bbi, "s1swp", "s2swp", k_w)
            zbuild("v", cre, cim, "s1far", "s2far", k_v)
            zbuild("far", cre, cim, "s1far", "s2far", k_f)
            zbuild("fsw", cre, cim, "s1fsw", "s2fsw", k_f)
            msk = maskf if d == 0 else maskb
            for g in range(NG):
                gs = slice(g * 128, (g + 1) * 128)
                st = stage[g % 2]
                p1 = ps_next()
                kb.op("pe", lambda e: e.transpose(p1.ap[:, 0:128], Z["wnat"].ap[:, gs], ident_f.ap), [Z["wnat"], ident_f], [p1])
                kb.op("pe", lambda e: e.transpose(p1.ap[:, 128:256], Z["wswp"].ap[:, gs], ident_f.ap), [Z["wswp"], ident_f], [p1])
                kb.op("pe", lambda e: e.matmul(p1.ap[:, 256:384], Z["wnat"].ap[:, gs], Z["v"].ap[:, gs], start=True, stop=True),
                      [Z["wnat"], Z["v"]], [p1])
                kb.op("act", lambda e: e.activation(st.ap[:, 0:256], p1.ap[:, 0:256], AF.Copy), [p1], [st])
                kb.op("act", lambda e: e.activation(st.ap[:, 256:384], Z["far"].ap[:, gs], AF.Copy), [Z["far"]], [st])
                kb.op("act", lambda e: e.activation(st.ap[:, 384:512], Z["fsw"].ap[:, gs], AF.Copy), [Z["fsw"]], [st])
                if d == 0:
                    kb.op("dve", lambda e: e.tensor_tensor(macc.ap[:, gs], p1.ap[:, 256:384], msk.ap, ALU.mult), [p1, msk], [macc])
                else:
                    kb.op("dve", lambda e: e.tensor_tensor(ztmp.ap[:, gs], p1.ap[:, 256:384], msk.ap, ALU.mult), [p1, msk], [ztmp])
                    kb.op("dve", lambda e: e.tensor_tensor(macc.ap[:, gs], macc.ap[:, gs], ztmp.ap[:, gs], ALU.add), [macc, ztmp], [macc])
                kb.dma("sp", X.s5w[d * NG + g], st.ap, reads=[st])
        for g in range(NG):
            gs = slice(g * 128, (g + 1) * 128)
            kb.op("dve", lambda e, g=g, gs=gs: e.scalar_tensor_tensor(macc.ap[:, gs], ident_f.ap, dtab.ap[:, g:g + 1], macc.ap[:, gs],
                                                                ALU.mult, ALU.add), [ident_f, dtab, macc], [macc])
        kb.op("act", lambda e: e.activation(mstage.ap, macc.ap, AF.Copy), [macc], [mstage, Z["fsw"]])
        kb.dma("sp", X.s5m.rearrange("g p c -> p g c"), mstage.ap.rearrange("p (g c) -> p g c", c=128), reads=[mstage])
        phi2 = ptile("phi2", [128, 64])
        sg2 = ptile("sg2", [128, 2])
        kb.op("dve", lambda e: e.tensor_scalar(phi2.ap, phi.ap, 1.0 / TWO_PI, None, ALU.mult), [phi], [phi2])
        kb.op("dve", lambda e: e.tensor_scalar(sg2.ap, misc.ap[:, 5:7], TWO_PI, None, ALU.mult), [misc], [sg2])
        kb.barrier()
        zw = Z["wswp"].ap
        assert nsb <= 600
        tsets = [(rang, rtf, rti, T(zw[:, 1800:1800 + nsb])),
                 (T(zw[:, 0:nsb]), T(zw[:, 600:600 + nsb]), T(zw.bitcast(I32)[:, 1200:1200 + nsb]), T(Z["v"].ap[:, 0:nsb]))]
        for item in range(64):
            d = item // NG
            rs = rstage[item % 2]
            r3 = rs.ap.rearrange("p (t s) -> p t s", t=2)
            rang_, rtf_, rti_, rsq_ = tsets[item % 2]
            a = rang_.ap
            kb.op("dve", lambda e: e.tensor_scalar(a, sbidx.ap, phi2.ap[:, item:item + 1], None, ALU.mult), [sbidx, phi2], [rang_])
            kb.op("dve", lambda e: e.tensor_copy(rti_.ap, a), [rang_], [rti_])
            kb.op("dve", lambda e: e.tensor_copy(rtf_.ap, rti_.ap), [rti_], [rtf_])
            kb.op("dve", lambda e: e.tensor_tensor(a, a, rtf_.ap, ALU.subtract), [rang_, rtf_], [rang_])
            kb.op("dve", lambda e: e.tensor_scalar(a, a, 0.4999999, -0.4999999, ALU.min, ALU.max), [rang_], [rang_])
            kb.op("act", lambda e: e.activation(r3[:, 1, :], a, AF.Sin, scale=sg2.ap[:, d:d + 1]), [rang_, sg2], [rs])
            kb.op("act", lambda e: e.activation(rsq_.ap, a, AF.Sin, scale=math.pi), [rang_], [rsq_])
            kb.op("act", lambda e: e.activation(rsq_.ap, rsq_.ap, AF.Square), [rsq_], [rsq_])
            kb.op("act", lambda e: e.activation(r3[:, 0, :], rsq_.ap, AF.Copy, scale=-2.0, bias=1.0), [rsq_], [rs])
            kb.dma("sp", X.rot[item].rearrange("p t s -> p (t s)"), rs.ap, reads=[rs])
        if "s5w" in X.dbg:
            pass
        kb.barrier()


def build_main(X):
    nc, kb, I, CI, sbt, cdma = X.nc, X.kb, X.I, X.CI, X.sbt, X.cdma
    psum, ident_f, ident_b, misc = X.psum, X.ident_f, X.ident_b, X.misc
    ropec, ropes, dmT, wtab, gfin, st0, umeta, rho = X.ropec, X.ropes, X.dmT, X.wtab, X.gfin, X.st0, X.umeta, X.rho
    wS, wM, s5w, s5m, rot = X.wS, X.wM, X.s5w, X.s5m, X.rot
    NP_, SP_, NS_, SS_ = X.NP_, X.SP_, X.NS_, X.SS_
    nch_max, nsb_max = X.nch_max, X.nsb_max
    top = X.top
    dbg = X.dbg
    seqs = [("p", i, SP_) for i in range(NP_)] + [("s", i, SS_) for i in range(NS_)]
    smax = max(s[2] for s in seqs)
    ngr_max = smax // 512
    nr_max = smax // NT

    uscrA = X.dscr("uscrA", [NG, GC, NT, nsb_max], BF16)
    uscr = X.dscr("uscr", [NG, NT, GC, nsb_max], BF16)
    yscr = X.dscr("yscr", [NG, 128, nr_max], BF16)
    yscrB = X.dscr("yscrB", [NG, GC, NT, nr_max], BF16)
    relay_u = T(None); relay_y = T(None)
    kscr = X.dscr("kscr", [nch_max, 128, 512], BF16)
    vscr = X.dscr("vscr", [nch_max, 128, H * DV], BF16)
    ntscr = X.dscr("ntscr", [ngr_max, 4, 128, 1024], BF16)
    ntst = [T(None) for _ in range(4)]
    kst = [T(None) for _ in range(4)]
    vst = [T(None) for _ in range(8)]
    stash = X.dscr("stash", [nch_max, 128, H * DV], BF16)

    def ctile(name, shape, dt=F32):
        return T(sbt(top, name, shape, dt)[:])

    def sub(t_ap):
        return T(t_ap)

    ring_t = sbt(top, "ring", [128, 4 * 4096], BF16)
    ring = [T(ring_t[:, i * 4096:(i + 1) * 4096]) for i in range(4)]
    ring_i = [0]

    def wload(src_ap, ncols=4096):
        t = ring[ring_i[0] % 4]
        ring_i[0] += 1
        kb.dma("sp", t.ap[:, 0:ncols], src_ap, writes=[t])
        return t

    xg_t = sbt(top, "xg", [128, 4 * D], F32)
    _xg0 = [T(xg_t[:, c * D:(c + 1) * D]) for c in range(4)]
    xg = [_xg0, _xg0]
    xs_t = sbt(top, "xs", [128, 2 * D], BF16)
    _xs2 = [T(xs_t[:, c * D:(c + 1) * D]) for c in range(2)]
    xs = [_xs2[0], _xs2[1], _xs2[0], _xs2[1]]
    nT_t = sbt(top, "nT", [128, KT * 512], BF16)
    nTq = [T(nT_t[:, b * 1024:(b + 1) * 1024]) for b in range(4)]
    nT3 = nT_t[:].rearrange("p (kt t) -> p kt t", t=512)
    ssq = [ctile(f"ssq{c}", [128, 1]) for c in range(4)]
    rsd = [ctile(f"rsd{c}", [128, 1]) for c in range(4)]
    stb = ctile("stb", [128, H * DV])
    stf = stb
    stf_bf = ctile("stf_bf", [128, H * DV], BF16)
    stb_bf = [ctile(f"stb_bf{i}", [128, H * DV], BF16) for i in range(2)]
    rt_t = ctile("rt_t", [128, 512]); rt_u1 = ctile("rt_u1", [128, 256]); rt_u2 = ctile("rt_u2", [128, 256])
    kr = [ctile(f"kr{c}", [128, 512], BF16) for c in range(4)]
    kw = [ctile(f"kw{c}", [128, 512], BF16) for c in range(4)]
    vv = [ctile(f"vv{c}", [128, H * DV], BF16) for c in range(4)]

    ps_rr = [0]

    def ps_next():
        b = ps_rr[0] % 8
        ps_rr[0] += 1
        return psum[b]

    def ps_pair():
        if ps_rr[0] % 2:
            ps_rr[0] += 1
        b = ps_rr[0] % 8
        ps_rr[0] += 2
        return psum[b], psum[b + 1]

    def bc4(ap2, h=4, two=2):
        return ap2.unsqueeze(1).unsqueeze(1).broadcast_to([128, h, two, 64])

    def bc3(ap2, h=4):
        return ap2.unsqueeze(1).broadcast_to([128, h, 64])

    def rope(ps, n, out_t, rows=128):
        cs = ropec.ap[0:rows, n * 64:(n + 1) * 64]
        sn = ropes.ap[0:rows, n * 64:(n + 1) * 64]
        x4 = ps.ap[0:rows, :].rearrange("p (h two f) -> p h two f", h=4, two=2)
        t4 = rt_t.ap[0:rows, :].rearrange("p (h two f) -> p h two f", h=4, two=2)
        o4 = out_t.ap[0:rows, :].rearrange("p (h two f) -> p h two f", h=4, two=2)
        u1 = rt_u1.ap[0:rows, :].rearrange("p (h f) -> p h f", h=4)
        u2 = rt_u2.ap[0:rows, :].rearrange("p (h f) -> p h f", h=4)
        cs4 = cs.unsqueeze(1).unsqueeze(1).broadcast_to([rows, 4, 2, 64])
        sn3 = sn.unsqueeze(1).broadcast_to([rows, 4, 64])
        kb.op("dve", lambda e: e.tensor_tensor(t4, x4, cs4, ALU.mult), [ps, ropec], [rt_t])
        kb.op("dve", lambda e: e.tensor_tensor(u1, x4[:, :, 1, :], sn3, ALU.mult), [ps, ropes], [rt_u1])
        kb.op("dve", lambda e: e.tensor_tensor(u2, x4[:, :, 0, :], sn3, ALU.mult), [ps, ropes], [rt_u2])
        kb.op("pool", lambda e: e.tensor_tensor(o4[:, :, 0, :], t4[:, :, 0, :], u1, ALU.subtract), [rt_t, rt_u1], [out_t])
        kb.op("pool", lambda e: e.tensor_tensor(o4[:, :, 1, :], t4[:, :, 1, :], u2, ALU.add), [rt_t, rt_u2, out_t], [out_t])

    def rmsnorm_to_nT(src, nchunks=4, rows=128):
        banks = [ps_next() for _ in range(4)]
        for c in range(nchunks):
            kb.op("act", lambda e, c=c: e.activation(xs[c].ap[0:rows, :], src[c].ap[0:rows, :], AF.Square, accum_out=ssq[c].ap[0:rows, :]),
                  [src[c]], [xs[c], ssq[c]])
            kb.op("act", lambda e, c=c: e.activation(rsd[c].ap[0:rows, :], ssq[c].ap[0:rows, :], AF.Sqrt, bias=EPS, scale=1.0 / D), [ssq[c]], [rsd[c]])
            kb.op("dve", lambda e, c=c: e.reciprocal(rsd[c].ap[0:rows, :], rsd[c].ap[0:rows, :]), [rsd[c]], [rsd[c]])
            kb.op("dve", lambda e, c=c: e.tensor_scalar(xs[c].ap[0:rows, :], src[c].ap[0:rows, :], rsd[c].ap[0:rows, 0:1], None, ALU.mult),
                  [src[c], rsd[c]], [xs[c]])

            def tr(e, c=c):
                last = None
                for kt in range(KT):
                    pb = banks[kt // 2].ap.bitcast(BF16)
                    col = (kt % 2) * 512 + c * rows
                    last = e.transpose(pb[:, col:col + rows], xs[c].ap[0:rows, kt * 128:(kt + 1) * 128], ident_b.ap[0:rows, 0:rows])
                return last
            kb.op("pe", tr, [xs[c], ident_b], banks)
        ncol = nchunks * rows
        for b in range(4):
            pb = banks[b].ap.bitcast(BF16).rearrange("p (k t) -> p k t", t=512)
            dst = nTq[b].ap.rearrange("p (k t) -> p k t", t=512)
            kb.op("act" if b % 2 == 0 else "dve",
                  (lambda e, pb=pb, dst=dst: e.activation(dst[:, :, 0:ncol], pb[:, :, 0:ncol], AF.Copy)) if b % 2 == 0 else
                  (lambda e, pb=pb, dst=dst: e.tensor_copy(dst[:, :, 0:ncol], pb[:, :, 0:ncol])),
                  [banks[b]], [nTq[b]])

    def mm_tok(bank, slot, c, rows=128, ncols=512):
        s3 = slot.ap.rearrange("p (kt n) -> p kt n", n=512)

        def f(e):
            last = None
            for kt in range(KT):
                last = e.matmul(bank.ap[0:rows, 0:ncols], nT3[:, kt, c * rows:(c + 1) * rows], s3[:, kt, 0:ncols],
                                start=(kt == 0), stop=(kt == KT - 1))
            return last
        kb.op("pe", f, [slot] + nTq, [bank])

    def mm_feat(bank, slot, col0, ntok=512, width=512, nk=KT, rhs=None, rhs_tiles=None):
        s3 = slot.ap.rearrange("p (kt n) -> p kt n", n=width) if width else None

        def f(e):
            last = None
            for kt in range(nk):
                r = nT3[:, kt, 0:ntok] if rhs is None else rhs(kt)
                last = e.matmul(bank.ap[:, 0:ntok], s3[:, kt, col0:col0 + 128], r, start=(kt == 0), stop=(kt == nk - 1))
            return last
        kb.op("pe", f, [slot] + (nTq if rhs_tiles is None else rhs_tiles), [bank])

    mrow = NM
    cdma(xg[0][0], xg[0][0].ap[0:NM, :], I["meta"])
    rmsnorm_to_nT([xg[0][0]], nchunks=1, rows=NM)
    slot = wload(wS[0])
    for m in range(4):
        bk = ps_next()
        mm_feat(bk, slot, m * 128, ntok=NM)
        kb.op("act", lambda e, m=m, bk=bk: e.activation(umeta.ap[:, m * NM:(m + 1) * NM], bk.ap[:, 0:NM], AF.Copy), [bk], [umeta])
    slot = wload(wS[2])
    bk = ps_next()
    mm_tok(bk, slot, 0, rows=NM)
    rope(bk, nch_max, kr[0], rows=NM)
    kb.op("pool", lambda e: e.tensor_tensor(kw[0].ap[0:NM, :].rearrange("p (h d) -> p h d", h=4),
                                             kr[0].ap[0:NM, :].rearrange("p (h d) -> p h d", h=4),
                                             wtab.ap[0:NM, 24:28].unsqueeze(2).broadcast_to([NM, 4, 128]), ALU.mult),
          [kr[0], wtab], [kw[0]])
    for j in range(2):
        slot = wload(wS[3 + j])
        bk = ps_next()
        mm_tok(bk, slot, 0, rows=NM)
        kb.op("act", lambda e, j=j, bk=bk: e.activation(vv[0].ap[0:NM, j * 512:(j + 1) * 512], bk.ap[0:NM, :], AF.Copy), [bk], [vv[0]])
    b0, b1 = ps_pair()
    for h in range(H):
        bk = b0 if h < 2 else b1
        kb.op("pe", lambda e, h=h, bk=bk: e.matmul(bk.ap[:, (h % 2) * 256:(h % 2 + 1) * 256], kw[0].ap[0:NM, h * 128:(h + 1) * 128],
                                                 vv[0].ap[0:NM, h * 256:(h + 1) * 256], start=True, stop=True), [kw[0], vv[0]], [bk])
    kb.op("act", lambda e: e.activation(st0.ap[:, 0:512], b0.ap, AF.Copy), [b0], [st0])
    kb.op("act", lambda e: e.activation(st0.ap[:, 512:1024], b1.ap, AF.Copy), [b1], [st0])
    st0d = X.dscr("st0d", [128, H * DV], F32)
    kb.dma("pool", st0d, st0.ap, reads=[st0])
    X.dbg_out("umeta", umeta.ap, [128, 4 * NM], [umeta], BF16)
    X.dbg_out("st0", st0.ap, [128, H * DV], [st0])
    kb.barrier()


    def s5_stage(S, nsb, nr):
        with ExitStack() as ph:
            def ptile(name, shape, dt=F32):
                return T(sbt(ph, name, shape, dt)[:], key="s5_" + name)
            mall = ptile("mall", [128, NG * 128], BF16)
            kb.dma("sp", mall.ap.rearrange("p (g c) -> p g c", c=128), s5m.rearrange("g p c -> p g c"), writes=[mall])
            ug = [ptile(f"ug{i}", [128, nsb], BF16) for i in range(2)]
            wi = [ptile(f"wi{i}", [128, 512], BF16) for i in range(4)]
            rt = [ptile(f"rt{i}", [128, 2 * nsb_max]) for i in range(4)]
            t1s = [ptile(f"s5t1_{i}", [128, nsb]) for i in range(2)]; t2s = [ptile(f"s5t2_{i}", [128, nsb]) for i in range(2)]
            zins = [ptile(f"zin{i}", [128, nsb]) for i in range(2)]; zzs = [ptile(f"zz{i}", [128, nsb]) for i in range(2)]
            zc = [ptile(f"zc{i}", [128, nsb], BF16) for i in range(2)]
            zs = [ptile(f"zs{i}", [128, nsb], BF16) for i in range(2)]
            yst = [ptile(f"yst{i}", [128, nr], BF16) for i in range(2)]
            u2T = T(None)
            for g in range(NG):
                kb.dma("sp", uscr[g, :, :, 0:nsb], uscrA[g, :, :, 0:nsb].rearrange("c i s -> i c s"), writes=[u2T], owner=relay_u, group=True)
            yAT = [T(None) for _ in range(NG)]
            yBT = T(None)
            items = [(g, d) for g in range(NG) for d in (1, 0)]

            def bufs(ic):
                g, d = items[ic]
                p = ic % 2
                return dict(g=g, d=d, item=d * NG + g, u=ug[g % 2], pY=psum[g % 2], w=wi[ic % 4], r=rt[ic % 4], t1=t1s[p], t2=t2s[p], zin=zins[p], zz=zzs[p],
                            c_t=zc[p], s_t=zs[p], pX=psum[2 + 3 * p], pYy=psum[3 + 3 * p], pM=psum[4 + 3 * p])

            def P_pe(ic):
                b = bufs(ic)
                g, d, item, u, pY, w, r = b["g"], b["d"], b["item"], b["u"], b["pY"], b["w"], b["r"]
                pX, pYy, pM = b["pX"], b["pYy"], b["pM"]
                if d == 1:
                    kb.dma("sp", u.ap, uscr[g, :, :, 0:nsb].rearrange("i c s -> (i c) s"), reads=[u2T], writes=[u])
                    kb.op("pe", lambda e: e.matmul(pY.ap[:, 0:nr], mall.ap[:, g * 128:(g + 1) * 128], u.ap[:, 2:nsb], start=True, stop=False), [mall, u], [pY])
                kb.dma("sp", w.ap, s5w[item], writes=[w])
                kb.dma("sp", r.ap.rearrange("p (t s) -> p t s", t=2)[:, :, 0:nsb], rot[item][:, :, 0:nsb], writes=[r])
                kb.op("pe", lambda e: e.matmul(pX.ap[:, 0:nr], w.ap[:, 0:128], u.ap[:, 2:nsb], start=True, stop=True), [w, u], [pX])
                kb.op("pe", lambda e: e.matmul(pYy.ap[:, 0:nr], w.ap[:, 128:256], u.ap[:, 2:nsb], start=True, stop=True), [w, u], [pYy])
                if d == 0:
                    def fm(e):
                        e.matmul(pM.ap[:, 0:2], w.ap[:, 0:128], u.ap[:, 0:2], start=True, stop=True)
                        return e.matmul(pM.ap[:, 2:4], w.ap[:, 128:256], u.ap[:, 0:2], start=True, stop=True)
                    kb.op("pe", fm, [w, u], [pM])

            def P_dve(ic):
                b = bufs(ic)
                d, r = b["d"], b["r"]
                t1, t2, zin, pX, pYy, pM = b["t1"], b["t2"], b["zin"], b["pX"], b["pYy"], b["pM"]
                cosT = r.ap[:, 0:nsb]
                sinT = r.ap[:, nsb_max:nsb_max + nsb]
                if d == 0:
                    kb.op("dve", lambda e: e.tensor_tensor(t1.ap[:, 0:2], pM.ap[:, 0:2], cosT[:, 0:2], ALU.mult), [pM, r], [t1])
                    kb.op("dve", lambda e: e.tensor_tensor(t2.ap[:, 0:2], pM.ap[:, 2:4], sinT[:, 0:2], ALU.mult), [pM, r], [t2])
                kb.op("dve", lambda e: e.tensor_tensor(t1.ap[:, 2:nsb], pX.ap[:, 0:nr], cosT[:, 2:nsb], ALU.mult), [pX, r, t1], [t1])
                kb.op("dve", lambda e: e.tensor_tensor(t2.ap[:, 2:nsb], pYy.ap[:, 0:nr], sinT[:, 2:nsb], ALU.mult), [pYy, r, t2], [t2])
                lo = 0 if d == 0 else 2
                kb.op("dve", lambda e: e.tensor_tensor(zin.ap[:, lo:nsb], t1.ap[:, lo:nsb], t2.ap[:, lo:nsb], ALU.add), [t1, t2], [zin])

            def Q_ew(ic):
                b = bufs(ic)
                d, item, r = b["d"], b["item"], b["r"]
                zin, zz, c_t, s_t = b["zin"], b["zz"], b["c_t"], b["s_t"]
                cosT = r.ap[:, 0:nsb]
                sinT = r.ap[:, nsb_max:nsb_max + nsb]
                lo = 0 if d == 0 else 2
                coef = rho.ap[:, item:item + 1].broadcast_to([128, nsb - lo])
                if d == 0:
                    kb.op("dve", lambda e: e.tensor_tensor_scan(zz.ap[:, 0:nsb], coef, zin.ap[:, 0:nsb], 0.0, ALU.mult, ALU.add), [zin, rho], [zz])
                else:
                    kb.op("dve", lambda e: e.tensor_tensor_scan(zz.ap[:, 2:nsb][:, ::-1], coef, zin.ap[:, 2:nsb][:, ::-1], 0.0, ALU.mult, ALU.add),
                          [zin, rho], [zz])
                kb.op("pool", lambda e: e.tensor_tensor(s_t.ap[:, lo:nsb], zz.ap[:, lo:nsb], sinT[:, lo:nsb], ALU.mult), [zz, r], [s_t])
                kb.op("pool", lambda e: e.tensor_tensor(c_t.ap[:, lo:nsb], zz.ap[:, lo:nsb], cosT[:, lo:nsb], ALU.mult), [zz, r], [c_t])

            def Q_pe(ic):
                b = bufs(ic)
                g, d, pY, w = b["g"], b["d"], b["pY"], b["w"]
                c_t, s_t = b["c_t"], b["s_t"]
                if d == 0:
                    def ff(e):
                        e.matmul(pY.ap[:, 0:nr], w.ap[:, 256:384], c_t.ap[:, 1:nsb - 1], start=False, stop=False)
                        return e.matmul(pY.ap[:, 0:nr], w.ap[:, 384:512], s_t.ap[:, 1:nsb - 1], start=False, stop=True)
                    kb.op("pe", ff, [w, c_t, s_t], [pY])
                    ys_ = yst[g % 2]
                    kb.op("act", lambda e: e.activation(ys_.ap, pY.ap[:, 0:nr], AF.Gelu), [pY], [ys_])
                    kb.dma("act", yscr[g, :, 0:nr], ys_.ap, reads=[ys_], writes=[yAT[g]], owner=ys_)
                else:
                    def fb(e):
                        e.matmul(pY.ap[:, 0:nr - 1], w.ap[:, 256:384], c_t.ap[:, 3:nsb], start=False, stop=False)
                        return e.matmul(pY.ap[:, 0:nr - 1], w.ap[:, 384:512], s_t.ap[:, 3:nsb], start=False, stop=False)
                    kb.op("pe", fb, [w, c_t, s_t], [pY])

            P_pe(0)
            P_dve(0)
            for ic in range(len(items)):
                if ic + 1 < len(items):
                    P_pe(ic + 1)
                Q_ew(ic)
                if ic + 1 < len(items):
                    P_dve(ic + 1)
                Q_pe(ic)
            for g in range(NG):
                kb.dma("sp", yscrB[g, :, :, 0:nr], yscr[g, :, 0:nr].rearrange("(i c) s -> c i s", c=GC), reads=[yAT[g]], writes=[yBT], owner=relay_y, group=True)
            kb.barrier()


    def pass_b(xin, yout, S, ngr):
        with ExitStack() as ph:
            def ptile(name, shape, dt=F32):
                return T(sbt(ph, name, shape, dt)[:], key="pb_" + name)
            qr = [ptile(f"qr{c}", [128, 512], BF16) for c in range(4)]
            _sg2 = [ptile(f"sg{c}", [128, H * DV], BF16) for c in range(2)]
            sg = [_sg2[0], _sg2[1], _sg2[0], _sg2[1]]
            qf = [ptile(f"qf{i}", [128, 512], BF16) for i in range(2)]
            qb = [ptile(f"qb{i}", [128, 512], BF16) for i in range(2)]
            kf = kw
            qkT = [ptile(f"qkT{i}", [128, 1024], BF16) for i in range(2)]
            qfbT = [ptile(f"qfbT{i}", [128, 1024], BF16) for i in range(2)]
            stf_bfs = [stf_bf, ptile("stf_bf1", [128, H * DV], BF16)]
            PT = [ptile(f"PT{i}", [128, 512], BF16) for i in range(2)]
            og = [ptile(f"og{i}", [128, H * DV], BF16) for i in range(2)]
            ssh_t = [sbt(ph, f"ssh{i}", [128, 4], F32) for i in range(2)]
            ssh = [[T(ssh_t[i][:, h:h + 1]) for h in range(H)] for i in range(2)]
            rsh = [ptile(f"rsh{i}", [128, 4]) for i in range(2)]
            ogT_t = sbt(ph, "ogT", [128, KT * 512], BF16)
            ogT3 = ogT_t[:].rearrange("p (kt t) -> p kt t", t=512)
            ogTc = [T(ogT3[:, :, c * 128:(c + 1) * 128]) for c in range(4)]
            mixT = [ptile(f"mixT{m}", [128, 512], BF16) for m in range(8)]
            sga = [ptile("sga", [128, 512], BF16)] * 2
            sgb = [ptile("sgb", [128, 512], BF16)] * 2
            sag = [ptile("sag", [128, 512], BF16)] * 2
            ta = [ptile("ta", [128, 512], BF16)] * 2
            tb = [ptile("tb", [128, 512], BF16)] * 2
            actT = [ptile(f"actT{j}", [128, 512], BF16) for j in range(FT)]
            sgt = [ptile(f"sgt{i}", [128, 512], BF16) for i in range(2)]
            ytl = [ptile("ytl", [128, 4 * 512], BF16)] * 2

            kb.dma("pool", stf.ap, st0d, writes=[stf])
            kb.op("act", lambda e: e.activation(stf_bf.ap, stf.ap, AF.Copy), [stf], [stf_bf])
            xbufB = [xg[0], [st0] + [ptile(f"x2b{c}", [128, D]) for c in range(3)]]

            def load_x(gi):
                xb_ = xbufB[gi % 2]
                for c in range(4):
                    kb.dma("pool", xb_[c].ap, xin[gi * 512 + c * 128: gi * 512 + (c + 1) * 128, :], writes=[xb_[c]])

            xst = [[T(None, key=f"pb_xst{b}_{c}") for c in range(4)] for b in range(2)]
            load_x(0)
            for gi in range(ngr):
                xb = xbufB[gi % 2]
                yt = ytl[gi % 2]
                for kt in range(4):
                    kb.dma("sp", yt.ap[:, kt * 512:(kt + 1) * 512].rearrange("p (i s) -> p i s", i=NT),
                           yscrB[kt * 8:(kt + 1) * 8, :, :, gi * 64:(gi + 1) * 64].rearrange("g c i s -> (g c) i s"),
                           writes=[yt], group=True)
                if gi == 0:
                    for b in range(4):
                        kb.dma("pool", nTq[b].ap, ntscr[gi, b], writes=[nTq[b]])
                slot = wload(wS[1])
                qbk = []
                for c in range(4):
                    bk = ps_next()
                    mm_tok(bk, slot, c)
                    rope(bk, gi * 4 + c, qr[c])
                for c in range(4):
                    kb.dma("sp", kr[c].ap, kscr[gi * 4 + c], writes=[kr[c]])
                    kb.dma("sp", vv[c].ap, vscr[gi * 4 + c], writes=[vv[c]])
                slot_g = [wload(wS[5]), wload(wS[6])]
                def st_A(c):
                    n = gi * 4 + c
                    p2 = c % 2
                    sbb = stb_bf[p2]
                    kb.dma("pool", sbb.ap, stash[n], writes=[sbb])

                    def sc3(src, dst, col):
                        kb.op("pool", lambda e: e.tensor_tensor(dst.ap.rearrange("p (h d) -> p h d", h=4), src.ap.rearrange("p (h d) -> p h d", h=4),
                                                                 wtab.ap[:, col:col + 4].unsqueeze(2).broadcast_to([128, 4, 128]), ALU.mult),
                              [src, wtab], [dst])
                    sc3(qr[c], qf[p2], 0)
                    sc3(qr[c], qb[p2], 8)
                    sc3(kr[c], kf[p2], 4)
                    bA = ps_next(); bB = ps_next()

                    def trq(e):
                        last = None
                        pa = bA.ap.bitcast(BF16); pb_ = bB.ap.bitcast(BF16)
                        for h in range(H):
                            hs = slice(h * 128, (h + 1) * 128)
                            e.transpose(pa[:, h * 128:(h + 1) * 128], qr[c].ap[:, hs], ident_b.ap)
                            e.transpose(pa[:, 512 + h * 128:512 + (h + 1) * 128], kr[c].ap[:, hs], ident_b.ap)
                            e.transpose(pb_[:, h * 128:(h + 1) * 128], qf[p2].ap[:, hs], ident_b.ap)
                            last = e.transpose(pb_[:, 512 + h * 128:512 + (h + 1) * 128], qb[p2].ap[:, hs], ident_b.ap)
                        return last
                    kb.op("pe", trq, [qr[c], kr[c], qf[p2], qb[p2], ident_b], [bA, bB])
                    kb.op("dve", lambda e: e.tensor_copy(qkT[p2].ap, bA.ap.bitcast(BF16)), [bA], [qkT[p2]])
                    kb.op("dve", lambda e: e.tensor_copy(qfbT[p2].ap, bB.ap.bitcast(BF16)), [bB], [qfbT[p2]])

                def st_B(c):
                    p2 = c % 2
                    bS = ps_next()

                    def scores(e):
                        last = None
                        for h in range(H):
                            hs = slice(h * 128, (h + 1) * 128)
                            last = e.matmul(bS.ap[:, hs], qkT[p2].ap[:, 512 + h * 128:512 + (h + 1) * 128], qkT[p2].ap[:, hs], start=True, stop=True)
                        return last
                    kb.op("pe", scores, [qkT[p2]], [bS])
                    kb.op("dve", lambda e: e.tensor_tensor(PT[p2].ap, bS.ap, dmT.ap, ALU.mult), [bS, dmT], [PT[p2]])

                def st_G(c):
                    for j in range(2):
                        bk = ps_next()
                        mm_tok(bk, slot_g[j], c)
                        kb.op("act", lambda e: e.activation(sg[c].ap[:, j * 512:(j + 1) * 512], bk.ap, AF.Silu), [bk], [sg[c]])

                def st_K(c):
                    p2 = c % 2
                    k0, k1 = ps_pair()

                    def kvmm(e):
                        last = None
                        for h in range(H):
                            reg = (k0 if h < 2 else k1).ap[:, (h % 2) * 256:(h % 2 + 1) * 256]
                            last = e.matmul(reg, kf[p2].ap[:, h * 128:(h + 1) * 128], vv[c].ap[:, h * 256:(h + 1) * 256], start=True, stop=True)
                        return last
                    kb.op("pe", kvmm, [kf[p2], vv[c]], [k0, k1])
                    for h in range(H):
                        bk = k0 if h < 2 else k1
                        kb.op("dve", lambda e: e.scalar_tensor_tensor(
                            stf.ap[:, h * 256:(h + 1) * 256], stf.ap[:, h * 256:(h + 1) * 256], wtab.ap[:, 16 + h:17 + h],
                            bk.ap[:, (h % 2) * 256:(h % 2 + 1) * 256], ALU.mult, ALU.add), [stf, wtab, bk], [stf])
                    nxt = stf_bfs[(gi * 4 + c + 1) % 2]
                    kb.op("dve", lambda e: e.tensor_copy(nxt.ap, stf.ap), [stf], [nxt])

                def st_O(c):
                    p2 = c % 2
                    sbb = stb_bf[p2]
                    cur = stf_bfs[(gi * 4 + c) % 2]
                    o0, o1 = ps_pair()

                    def omm(e):
                        last = None
                        for h in range(H):
                            reg = (o0 if h < 2 else o1).ap[:, (h % 2) * 256:(h % 2 + 1) * 256]
                            hs = slice(h * 128, (h + 1) * 128)
                            vs = slice(h * 256, (h + 1) * 256)
                            e.matmul(reg, PT[p2].ap[:, hs], vv[c].ap[:, vs], start=True, stop=False)
                            e.matmul(reg, qfbT[p2].ap[:, hs], cur.ap[:, vs], start=False, stop=False)
                            last = e.matmul(reg, qfbT[p2].ap[:, 512 + h * 128:512 + (h + 1) * 128], sbb.ap[:, vs], start=False, stop=True)
                        return last
                    kb.op("pe", omm, [PT[p2], vv[c], qfbT[p2], cur, sbb], [o0, o1])
                    for h in range(H):
                        ob = o0 if h < 2 else o1
                        kb.op("act", lambda e: e.activation(og[p2].ap[:, h * 256:(h + 1) * 256], ob.ap[:, (h % 2) * 256:(h % 2 + 1) * 256],
                                                            AF.Square, accum_out=ssh[p2][h].ap), [ob], [og[p2], ssh[p2][h]])
                    kb.op("act", lambda e: e.activation(rsh[p2].ap, ssh_t[p2][:], AF.Sqrt, bias=EPS, scale=1.0 / DV), ssh[p2], [rsh[p2]])
                    kb.op("dve", lambda e: e.reciprocal(rsh[p2].ap, rsh[p2].ap), [rsh[p2]], [rsh[p2]])
                    for h in range(H):
                        ob = o0 if h < 2 else o1
                        kb.op("dve", lambda e: e.scalar_tensor_tensor(
                            og[p2].ap[:, h * 256:(h + 1) * 256], ob.ap[:, (h % 2) * 256:(h % 2 + 1) * 256], rsh[p2].ap[:, h:h + 1],
                            sg[c].ap[:, h * 256:(h + 1) * 256], ALU.mult, ALU.mult), [ob, rsh[p2], sg[c]], [og[p2]])

                def st_R(c):
                    p2 = c % 2
                    bT = ps_next()

                    def trog(e):
                        last = None
                        pt_ = bT.ap.bitcast(BF16)
                        for ft in range(KT):
                            last = e.transpose(pt_[:, ft * 128:(ft + 1) * 128], og[p2].ap[:, ft * 128:(ft + 1) * 128], ident_b.ap)
                        return last
                    kb.op("pe", trog, [og[p2], ident_b], [bT])
                    kb.op("dve", lambda e: e.tensor_copy(ogTc[c].ap, bT.ap.bitcast(BF16).rearrange("p (kt t) -> p kt t", t=128)),
                          [bT], [ogTc[c]])

                st_A(0)
                for c in range(4):
                    if c + 1 < 4:
                        st_A(c + 1)
                    st_B(c)
                    st_G(c)
                    st_K(c)
                    if c >= 1:
                        st_R(c - 1)
                    st_O(c)
                st_R(3)
                if gi + 1 < ngr:
                    load_x(gi + 1)
                for mt in range(8):
                    slot = wload(wM[mt])
                    a3 = slot.ap[:, 0:3072].rearrange("p (kt c) -> p kt c", c=384)
                    g3 = slot.ap[:, 3072:4096].rearrange("p (kt c) -> p kt c", c=256)
                    m2 = mt % 2
                    bbk, gak, gbk, avk, agk = ps_next(), ps_next(), ps_next(), ps_next(), ps_next()

                    def mm8(e, bank, c0, rhs):
                        last = None
                        for kt in range(KT):
                            last = e.matmul(bank.ap, a3[:, kt, c0:c0 + 128], rhs(kt), start=(kt == 0), stop=(kt == KT - 1))
                        return last

                    def mm4(e, bank, c0):
                        last = None
                        for kt in range(4):
                            last = e.matmul(bank.ap, g3[:, kt, c0:c0 + 128], yt.ap[:, kt * 512:(kt + 1) * 512], start=(kt == 0), stop=(kt == 3))
                        return last
                    kb.op("pe", lambda e: mm8(e, bbk, 0, lambda kt: ogT3[:, kt, :]), [slot] + ogTc, [bbk])
                    kb.op("pe", lambda e: mm8(e, gak, 128, lambda kt: nT3[:, kt, :]), [slot] + nTq, [gak])
                    kb.op("pe", lambda e: mm8(e, gbk, 256, lambda kt: nT3[:, kt, :]), [slot] + nTq, [gbk])
                    kb.op("pe", lambda e: mm4(e, avk, 0), [slot, yt], [avk])
                    kb.op("pe", lambda e: mm4(e, agk, 128), [slot, yt], [agk])
                    kb.op("act", lambda e: e.activation(sga[m2].ap, gak.ap, AF.Sigmoid), [gak], [sga[m2]])
                    kb.op("act", lambda e: e.activation(sgb[m2].ap, gbk.ap, AF.Sigmoid), [gbk], [sgb[m2]])
                    kb.op("act", lambda e: e.activation(sag[m2].ap, agk.ap, AF.Sigmoid), [agk], [sag[m2]])
                    kb.op("dve", lambda e: e.tensor_tensor(ta[m2].ap.rearrange("p (s i) -> p i s", i=NT),
                                                           avk.ap.rearrange("p (i s) -> p i s", i=NT),
                                                           sag[m2].ap.rearrange("p (i s) -> p i s", i=NT), ALU.mult), [avk, sag[m2]], [ta[m2]])
                    kb.op("pool", lambda e: e.tensor_tensor(ta[m2].ap, ta[m2].ap, sga[m2].ap, ALU.mult), [ta[m2], sga[m2]], [ta[m2]])
                    kb.op("dve", lambda e: e.tensor_tensor(tb[m2].ap, bbk.ap, sgb[m2].ap, ALU.mult), [bbk, sgb[m2]], [tb[m2]])
                    kb.op("pool", lambda e: e.tensor_tensor(mixT[mt].ap, ta[m2].ap, tb[m2].ap, ALU.add), [ta[m2], tb[m2]], [mixT[mt]])
                so = [wload(wS[7]), wload(wS[8])]
                for c in range(4):
                    b0, b1 = ps_pair()
                    for hf, bk in ((0, b0), (1, b1)):
                        s3 = so[hf].ap.rearrange("p (kt n) -> p kt n", n=512)

                        def wo(e, bk=bk, s3=s3, c=c):
                            last = None
                            for kt in range(KT):
                                last = e.matmul(bk.ap, mixT[kt].ap[:, c * 128:(c + 1) * 128], s3[:, kt, :], start=(kt == 0), stop=(kt == KT - 1))
                            return last
                        kb.op("pe", wo, [so[hf]] + mixT, [bk])
                        kb.op("dve", lambda e, bk=bk, hf=hf, c=c: e.tensor_tensor(xb[c].ap[:, hf * 512:(hf + 1) * 512], xb[c].ap[:, hf * 512:(hf + 1) * 512],
                                                                               bk.ap, ALU.add), [xb[c], bk], [xb[c]])
                rmsnorm_to_nT(xb)
                for b in range(11):
                    slot = wload(wS[9 + b])
                    for j in range(2):
                        gk, uk = ps_next(), ps_next()
                        mm_feat(gk, slot, j * 128)
                        mm_feat(uk, slot, 256 + j * 128)
                        st_ = sgt[j]
                        kb.op("act", lambda e, gk=gk, st_=st_: e.activation(st_.ap, gk.ap, AF.Silu), [gk], [st_])
                        kb.op("dve", lambda e, uk=uk, st_=st_, b=b, j=j: e.tensor_tensor(actT[2 * b + j].ap, uk.ap, st_.ap, ALU.mult),
                              [uk, st_], [actT[2 * b + j]])
                if gi + 1 < ngr:
                    for b in range(4):
                        kb.dma("pool", nTq[b].ap, ntscr[gi + 1, b], writes=[nTq[b]])
                for kg in range(3):
                    k0_ = 8 * kg
                    nk = min(8, FT - k0_)
                    for hf in range(2):
                        slot = wload(wS[20 + 2 * kg + hf])
                        s3 = slot.ap.rearrange("p (kt n) -> p kt n", n=512)
                        for c in range(4):
                            bk = psum[2 * c + hf]

                            def f2(e, bk=bk, s3=s3, c=c, k0_=k0_, nk=nk):
                                last = None
                                for kk in range(nk):
                                    kt = k0_ + kk
                                    last = e.matmul(bk.ap, actT[kt].ap[:, c * 128:(c + 1) * 128], s3[:, kk, :], start=(kt == 0), stop=(kt == FT - 1))
                                return last
                            kb.op("pe", f2, [slot] + actT[k0_:k0_ + nk], [bk])
                for c in range(4):
                    for hf in range(2):
                        bk = psum[2 * c + hf]
                        kb.op("dve", lambda e, bk=bk, hf=hf, c=c: e.tensor_tensor(xb[c].ap[:, hf * 512:(hf + 1) * 512], xb[c].ap[:, hf * 512:(hf + 1) * 512],
                                                                               bk.ap, ALU.add), [xb[c], bk], [xb[c]])
                    kb.op("act", lambda e, c=c: e.activation(xs[c].ap, xb[c].ap, AF.Square, accum_out=ssq[c].ap), [xb[c]], [xs[c], ssq[c]])
                    kb.op("act", lambda e, c=c: e.activation(rsd[c].ap, ssq[c].ap, AF.Sqrt, bias=EPS, scale=1.0 / D), [ssq[c]], [rsd[c]])
                    kb.op("dve", lambda e, c=c: e.reciprocal(rsd[c].ap, rsd[c].ap), [rsd[c]], [rsd[c]])
                    kb.op("dve", lambda e, c=c: e.scalar_tensor_tensor(xb[c].ap, xb[c].ap, rsd[c].ap[:, 0:1], gfin.ap, ALU.mult, ALU.mult),
                          [xb[c], rsd[c], gfin], [xb[c]])
                    kb.dma("act", yout[gi * 512 + c * 128: gi * 512 + (c + 1) * 128, :], xb[c].ap, reads=[xb[c]], owner=xst[gi % 2][c], is_out=True)
            kb.barrier()

    for (kind, si, S) in seqs:
        xin = I["xp" if kind == "p" else "xs"][si]
        yout = (X.yp if kind == "p" else X.ys)[si]
        nch = S // CH
        ngr = S // 512
        nsb = (S + NM) // NT
        nr = S // NT
        with ExitStack() as ph:
            ust = [T(sbt(ph, f"ust{i}", [128, 512], BF16)[:], key=f"pa_ust{i}") for i in range(2)]
            uc = 0
            um4 = umeta.ap.rearrange("p (kt s i) -> p kt i s", kt=4, i=NT)
            for kt in range(4):
                for i in range(NT):
                    kb.dma("pool", uscrA[kt * 8:(kt + 1) * 8, :, i, 0:2].rearrange("g c s -> (g c) s"), um4[:, kt, i, :], reads=[umeta],
                           allow_slow_non_contiguous=True)
            xg2_t = sbt(ph, "xg2", [128, 4 * D], F32)
            xbuf = [xg[0], [T(xg2_t[:, c * D:(c + 1) * D], key=f"pa_xg2_{c}") for c in range(4)]]

            def load_xa(gi_):
                xb_ = xbuf[gi_ % 2]
                for c in range(4):
                    kb.dma("pool", xb_[c].ap, xin[gi_ * 512 + c * 128: gi_ * 512 + (c + 1) * 128, :], writes=[xb_[c]])

            kwA = [kw, [T(sbt(ph, f"kw2_{c}", [128, 512], BF16)[:]) for c in range(4)]]
            vvA = [vv, [T(sbt(ph, f"vv2_{c}", [128, H * DV], BF16)[:]) for c in range(4)]]
            st_first = [True]

            def chain(gi):
                kw_, vv_ = kwA[gi % 2], vvA[gi % 2]
                for c in range(3, -1, -1):
                    n = gi * 4 + c
                    sb_ = stb_bf[n % 2]
                    if st_first[0]:
                        kb.op("dve", lambda e: e.memset(stb.ap, 0.0), [], [stb])
                        kb.op("pool", lambda e: e.memset(sb_.ap, 0.0), [], [sb_])
                    else:
                        kb.op("act", lambda e: e.activation(sb_.ap, stb.ap, AF.Copy), [stb], [sb_])
                    st_first[0] = False
                    kb.dma("pool", stash[n], sb_.ap, reads=[sb_])
                    if n == 0:
                        break
                    b0, b1 = ps_pair()

                    def kvb(e):
                        last = None
                        for h in range(H):
                            bk = b0 if h < 2 else b1
                            last = e.matmul(bk.ap[:, (h % 2) * 256:(h % 2 + 1) * 256], kw_[c].ap[:, h * 128:(h + 1) * 128],
                                            vv_[c].ap[:, h * 256:(h + 1) * 256], start=True, stop=True)
                        return last
                    kb.op("pe", kvb, [kw_[c], vv_[c]], [b0, b1])
                    for h in range(H):
                        bk = b0 if h < 2 else b1
                        kb.op("dve", lambda e: e.scalar_tensor_tensor(
                            stb.ap[:, h * 256:(h + 1) * 256], stb.ap[:, h * 256:(h + 1) * 256], wtab.ap[:, 20 + h:21 + h],
                            bk.ap[:, (h % 2) * 256:(h % 2 + 1) * 256], ALU.mult, ALU.add), [stb, wtab, bk], [stb])

            load_xa(ngr - 1)
            pending = None
            for gi in range(ngr - 1, -1, -1):
                xb = xbuf[gi % 2]
                kw_, vv_ = kwA[gi % 2], vvA[gi % 2]
                if gi - 1 >= 0:
                    load_xa(gi - 1)
                rmsnorm_to_nT(xb)
                for b in range(4):
                    kb.dma("act", ntscr[gi, b], nTq[b].ap, reads=[nTq[b]], owner=ntst[b])
                slot = wload(wS[0])
                for m in range(4):
                    bk = ps_next()
                    mm_feat(bk, slot, m * 128)
                    us = ust[uc % 2]
                    uc += 1
                    kb.op("act", lambda e: e.activation(us.ap.rearrange("p (i s) -> p i s", i=NT),
                                                        bk.ap.rearrange("p (s i) -> p i s", i=NT), AF.Copy), [bk], [us])
                    kb.dma("act", uscrA[m * 8:(m + 1) * 8, :, :, 2 + gi * 64: 2 + (gi + 1) * 64].rearrange("g c i s -> (g c) i s"),
                           us.ap.rearrange("p (i s) -> p i s", i=NT), reads=[us])
                slot = wload(wS[2])
                for c in range(4):
                    bk = ps_next()
                    mm_tok(bk, slot, c)
                    rope(bk, gi * 4 + c, kr[c])
                    kb.dma("pool", kscr[gi * 4 + c], kr[c].ap, reads=[kr[c]], owner=kst[c])
                    kb.op("pool", lambda e: e.tensor_tensor(kw_[c].ap.rearrange("p (h d) -> p h d", h=4),
                                                             kr[c].ap.rearrange("p (h d) -> p h d", h=4),
                                                             wtab.ap[:, 12:16].unsqueeze(2).broadcast_to([128, 4, 128]), ALU.mult),
                          [kr[c], wtab], [kw_[c]])
                for j in range(2):
                    slot = wload(wS[3 + j])
                    for c in range(4):
                        bk = ps_next()
                        mm_tok(bk, slot, c)
                        kb.op("act", lambda e: e.activation(vv_[c].ap[:, j * 512:(j + 1) * 512], bk.ap, AF.Copy), [bk], [vv_[c]])
                for c in range(4):
                    kb.dma("act", vscr[gi * 4 + c], vv_[c].ap, reads=[vv_[c]], owner=vst[(gi % 2) * 4 + c])
                if pending is not None:
                    chain(pending)
                pending = gi
            chain(pending)
            kb.barrier()
        s5_stage(S, nsb, nr)
        pass_b(xin, yout, S, ngr)


_CACHE = {}


def _in_map(inp, consts, xp=None, xs=None):
    f32 = lambda a: np.ascontiguousarray(np.asarray(a, dtype=np.float32))
    m = {}
    if xp is not None:
        m["xp"] = f32(xp)
    if xs is not None:
        m["xs"] = f32(xs)
    m["meta"] = f32(inp["meta_tokens"])
    m["g_mix"] = f32(inp["norm_mix_g"]).reshape(D)
    m["w_in"] = f32(inp["w_in"]).reshape(D, 5632)
    m["lam_re"] = f32(inp["ssm_lam_re"]).reshape(2, NG, PS)
    m["lam_im"] = f32(inp["ssm_lam_im"]).reshape(2, NG, PS)
    m["log_step"] = f32(inp["ssm_log_step"]).reshape(2, NG)
    m["b_re"] = f32(inp["ssm_b_re"]).reshape(2, NG, PS, GC)
    m["b_im"] = f32(inp["ssm_b_im"]).reshape(2, NG, PS, GC)
    m["c_re"] = f32(inp["ssm_c_re"]).reshape(2, NG, GC, PS)
    m["c_im"] = f32(inp["ssm_c_im"]).reshape(2, NG, GC, PS)
    m["ssm_d"] = f32(inp["ssm_d"]).reshape(512)
    m["w_glu"] = f32(inp["w_ssm_glu"]).reshape(512, 2048)
    m["decay"] = f32(inp["ret_decay_logit"]).reshape(8)
    m["w_ro"] = f32(inp["w_ret_out"]).reshape(D, D)
    m["w_out"] = f32(inp["w_out"]).reshape(D, D)
    m["g_ffn"] = f32(inp["norm_ffn_g"]).reshape(D)
    m["w_f1"] = f32(inp["w_ffn_in"]).reshape(D, 2 * DFF)
    m["w_f2"] = f32(inp["w_ffn_out"]).reshape(DFF, D)
    m["g_fin"] = f32(inp["norm_final_g"]).reshape(D)
    for k, v in consts.items():
        m["c_" + k] = f32(v)
    return m


def kernel(**inputs):
    xp = np.asarray(inputs["x_prompt"], dtype=np.float32)
    xs = np.asarray(inputs["x_sample"], dtype=np.float32)
    n = 8
    npc = xp.shape[0] // n
    nsc = xs.shape[0] // n
    key = (npc, xp.shape[1], nsc, xs.shape[1])
    if key not in _CACHE:
        _CACHE[key] = build(npc, xp.shape[1], nsc, xs.shape[1])
    nc, consts, _ = _CACHE[key]
    in_maps = [_in_map(inputs, consts, xp=xp[c * npc:(c + 1) * npc], xs=xs[c * nsc:(c + 1) * nsc]) for c in range(n)]
    res = run_bass_kernel_spmd(nc, in_maps, core_ids=list(range(n)))
    yp = np.concatenate([np.asarray(r["yp"], dtype=np.float32) for r in res.results], axis=0)
    ys = np.concatenate([np.asarray(r["ys"], dtype=np.float32) for r in res.results], axis=0)
    return (yp, ys)
```
